# Optimizing a Trainium2 kernel written in Bass

```python
import math
import jax, jax.numpy as jnp
from jax import lax
import numpy as np

D_MODEL = 1024
BATCH = 4
SEQ = 4096
DEPTH = 2
DEC_BATCH = 128
DEC_SEQ = 4
PAST_LEN = 2048
PAGE_SIZE = 128

HEAD_DIM = 64
ATT_GROUPS = ((128, 1), (512, 4), (2048, 16))
HEADS_PER_GROUP = 4
N_ATT_HEADS = HEADS_PER_GROUP * len(ATT_GROUPS)
ATT_WIDTH = N_ATT_HEADS * HEAD_DIM
SSM_HEAD_DIM = 64
SSM_WIDTH = D_MODEL
SSM_HEADS = SSM_WIDTH // SSM_HEAD_DIM
SSM_STATE = 128
SSM_GROUPS = 2
SSM_HPG = SSM_HEADS // SSM_GROUPS
SSM_CONV = 4
SSM_CHUNK = 128
XBC_WIDTH = SSM_WIDTH + 2 * SSM_GROUPS * SSM_STATE
MIX_WIDTH = ATT_WIDTH + SSM_WIDTH
IN_SPLITS = (ATT_WIDTH, 2 * ATT_WIDTH, 3 * ATT_WIDTH, 3 * ATT_WIDTH + SSM_WIDTH, 3 * ATT_WIDTH + SSM_WIDTH + XBC_WIDTH)
IN_WIDTH = IN_SPLITS[-1] + SSM_HEADS
D_FF = 128 * int(math.ceil(8 * D_MODEL / 3 / 128))
FFN_CONV = 3
EPS = 1e-6

kernel_name = 'hymba_ssd_dilated_convffn_step'


def rmsnorm(x, g):
    xf = x.astype(jnp.float32)
    y = xf * lax.rsqrt(jnp.mean(xf * xf, axis=-1, keepdims=True) + EPS)
    return (y * g.astype(jnp.float32)).astype(x.dtype)


def alibi_slopes():
    h = jnp.arange(1, N_ATT_HEADS + 1, dtype=jnp.float32)
    return jnp.exp2(-8.0 * h / N_ATT_HEADS)


def softmax_lse(s):
    mx = jnp.max(s, axis=-1, keepdims=True)
    e = jnp.exp(s - mx)
    den = jnp.sum(e, axis=-1, keepdims=True)
    return e / den, (mx + jnp.log(den))[..., 0]


def causal_dwconv(x, prev, w, b):
    width = w.shape[0]
    L = x.shape[1]
    xp = jnp.concatenate([prev.astype(x.dtype), x], axis=1)
    y = xp[:, 0:L] * w[0]
    for i in range(1, width):
        y = y + xp[:, i:i + L] * w[i]
    return y + b, xp[:, L:]


def dilated_attn_prompt(q, k, v, slopes, window, dil):
    bsz, S, H, hd = q.shape
    band = window // dil
    span = band * dil
    lp = -(-S // span) * span
    m = lp // dil
    nb = m // band

    def to_blocks(t):
        t = jnp.pad(t, ((0, 0), (0, lp - S), (0, 0), (0, 0)))
        t = t.reshape(bsz, m, dil, H, hd).transpose(0, 2, 1, 3, 4)
        return t.reshape(bsz, dil, nb, band, H, hd)

    def with_prev(t):
        prev = jnp.pad(t, ((0, 0), (0, 0), (1, 0), (0, 0), (0, 0), (0, 0)))[:, :, :-1]
        return jnp.concatenate([prev, t], axis=3)

    qb = to_blocks(q)
    kb = with_prev(to_blocks(k))
    vb = with_prev(to_blocks(v))
    s = jnp.einsum('brnihd,brnjhd->brnhij', qb, kb).astype(jnp.float32) * (hd ** -0.5)
    step = (jnp.arange(band)[:, None] + band) - jnp.arange(2 * band)[None, :]
    valid = (step >= 0) & (step <= band)
    valid = valid[None] & ((jnp.arange(nb)[:, None, None] > 0) | (jnp.arange(2 * band)[None, None, :] >= band))
    bias = -slopes.astype(jnp.float32)[:, None, None] * (step * dil).astype(jnp.float32)
    s = jnp.where(valid[:, None], s + bias, -jnp.inf)
    p, lse = softmax_lse(s)
    o = jnp.einsum('brnhij,brnjhd->brnihd', p.astype(v.dtype), vb)
    o = o.reshape(bsz, dil, m, H, hd).transpose(0, 2, 1, 3, 4).reshape(bsz, lp, H, hd)[:, :S]
    lse = lse.transpose(0, 1, 2, 4, 3).reshape(bsz, dil, m, H).transpose(0, 2, 1, 3).reshape(bsz, lp, H)[:, :S]
    return o, lse


def dilated_attn_sample(q, k, v, kv_cache, slopes, window, dil):
    T, hd = q.shape[1], q.shape[-1]
    lc = kv_cache.shape[1]
    band = window // dil
    keys = jnp.concatenate([kv_cache[:, :, 0].astype(k.dtype), k], axis=1)
    vals = jnp.concatenate([kv_cache[:, :, 1].astype(v.dtype), v], axis=1)
    steps = jnp.arange(band + 1)
    idx = lc + jnp.arange(T)[:, None] - steps[None, :] * dil
    valid = idx >= 0
    idx = jnp.maximum(idx, 0)
    kg = keys[:, idx]
    vg = vals[:, idx]
    s = jnp.einsum('bthd,btjhd->bthj', q, kg).astype(jnp.float32) * (hd ** -0.5)
    bias = -slopes.astype(jnp.float32)[:, None] * (steps * dil).astype(jnp.float32)[None, :]
    s = jnp.where(valid[:, None, :], s + bias, -jnp.inf)
    p, lse = softmax_lse(s)
    o = jnp.einsum('bthj,btjhd->bthd', p.astype(v.dtype), vg)
    return o, lse


def ssd_scan(x, dt, a, bm, cm, h0):
    bsz, L, H, P = x.shape
    G, N = bm.shape[2], bm.shape[3]
    hpg = H // G
    f32 = jnp.float32
    q = min(SSM_CHUNK, L)
    lp = -(-L // q) * q
    nc = lp // q

    def padl(t):
        return jnp.pad(t, [(0, 0), (0, lp - L)] + [(0, 0)] * (t.ndim - 2))

    xc = padl(x.astype(f32)).reshape(bsz, nc, q, G, hpg, P)
    dtc = padl(dt).reshape(bsz, nc, q, G, hpg)
    bc = padl(bm.astype(f32)).reshape(bsz, nc, q, G, N)
    cc = padl(cm.astype(f32)).reshape(bsz, nc, q, G, N)
    acum = jnp.cumsum(dtc * a.reshape(G, hpg), axis=2)
    causal = jnp.tril(jnp.ones((q, q), bool))[:, :, None, None]
    seg = acum[:, :, :, None] - acum[:, :, None, :]
    decay = jnp.exp(jnp.where(causal, seg, -jnp.inf))
    xdt = xc * dtc[..., None]
    cb = jnp.einsum('bcign,bcjgn->bcijg', cc, bc)
    y_intra = jnp.einsum('bcijg,bcijgh,bcjghp->bcighp', cb, decay, xdt)
    decay_end = jnp.exp(acum[:, :, -1:] - acum)
    states = jnp.einsum('bcjgn,bcjgh,bcjghp->bcghpn', bc, decay_end, xdt)
    chunk_decay = jnp.exp(acum[:, :, -1])

    def step(h, inp):
        s_c, d_c = inp
        return h * d_c[..., None, None] + s_c, h

    h_last, h_prev = lax.scan(step, h0.reshape(bsz, G, hpg, P, N),
                              (jnp.moveaxis(states, 1, 0), jnp.moveaxis(chunk_decay, 1, 0)))
    h_prev = jnp.moveaxis(h_prev, 0, 1)
    y_inter = jnp.einsum('bcign,bcigh,bcghpn->bcighp', cc, jnp.exp(acum), h_prev)
    y = (y_intra + y_inter).reshape(bsz, lp, H, P)[:, :L]
    return y, h_last.reshape(bsz, H, P, N)


def hybrid_layer(x, lw, kv_caches, h0, conv_prev, ffn_prev):
    (norm_mix, w_in, q_norm, k_norm, conv_w, conv_b, dt_bias, a_log, d_skip, ssm_norm, w_out,
     norm_ffn, w_up, ffn_conv_w, ffn_conv_b, w_down) = lw
    bsz, L, _ = x.shape
    h = rmsnorm(x, norm_mix)
    proj = h @ w_in
    q, k, v, z, xbc, dt_raw = jnp.split(proj, IN_SPLITS, axis=-1)

    q = rmsnorm(q.reshape(bsz, L, N_ATT_HEADS, HEAD_DIM), q_norm)
    k = rmsnorm(k.reshape(bsz, L, N_ATT_HEADS, HEAD_DIM), k_norm)
    v = v.reshape(bsz, L, N_ATT_HEADS, HEAD_DIM)
    slopes = alibi_slopes().reshape(len(ATT_GROUPS), HEADS_PER_GROUP)
    outs, lses, new_kv = [], [], []
    for g, (win, dil) in enumerate(ATT_GROUPS):
        hs = slice(g * HEADS_PER_GROUP, (g + 1) * HEADS_PER_GROUP)
        qg, kg, vg = q[:, :, hs], k[:, :, hs], v[:, :, hs]
        if kv_caches is None:
            o, lse = dilated_attn_prompt(qg, kg, vg, slopes[g], win, dil)
            keep = min(win, L)
            new_kv.append(jnp.stack([kg[:, L - keep:], vg[:, L - keep:]], axis=2))
        else:
            o, lse = dilated_attn_sample(qg, kg, vg, kv_caches[g], slopes[g], win, dil)
            new_kv.append(jnp.stack([kg, vg], axis=2))
        outs.append(o)
        lses.append(lse)
    alpha = jax.nn.softmax(jnp.stack(lses, axis=0), axis=0)
    att = jnp.concatenate([o * alpha[g][..., None].astype(o.dtype) for g, o in enumerate(outs)], axis=2)
    att = att.reshape(bsz, L, ATT_WIDTH)

    xbc_c, conv_new = causal_dwconv(xbc, conv_prev, conv_w, conv_b)
    xbc_c = jax.nn.silu(xbc_c)
    xs, bm, cm = jnp.split(xbc_c, (SSM_WIDTH, SSM_WIDTH + SSM_GROUPS * SSM_STATE), axis=-1)
    dt = jax.nn.softplus(dt_raw.astype(jnp.float32) + dt_bias.astype(jnp.float32))
    a = -jnp.exp(a_log.astype(jnp.float32))
    xs = xs.reshape(bsz, L, SSM_HEADS, SSM_HEAD_DIM)
    y, h_last = ssd_scan(xs, dt, a, bm.reshape(bsz, L, SSM_GROUPS, SSM_STATE),
                         cm.reshape(bsz, L, SSM_GROUPS, SSM_STATE), h0)
    y = y + d_skip.astype(jnp.float32)[:, None] * xs.astype(jnp.float32)
    y = y.reshape(bsz, L, SSM_WIDTH) * jax.nn.silu(z.astype(jnp.float32))
    y = rmsnorm(y.astype(x.dtype), ssm_norm)

    x = x + jnp.concatenate([att, y], axis=-1) @ w_out

    u = rmsnorm(x, norm_ffn) @ w_up
    u, ffn_new = causal_dwconv(u, ffn_prev, ffn_conv_w, ffn_conv_b)
    gate, up = jnp.split(u, 2, axis=-1)
    x = x + (jax.nn.silu(gate) * up) @ w_down
    return x, new_kv, h_last, conv_new, ffn_new


def setup_inputs(seed: int = 0) -> dict:
    key = jax.random.key(seed)
    ks = jax.random.split(key, 24)
    f32 = jnp.float32

    def nrm(k, shape, scale):
        return jax.random.normal(k, shape, f32) * scale

    kv_shape = lambda win: (DEPTH, DEC_BATCH, min(win, PAST_LEN), 2, HEADS_PER_GROUP, HEAD_DIM)
    dt0 = jnp.exp(jax.random.uniform(ks[14], (DEPTH, SSM_HEADS), f32, math.log(1e-3), math.log(1e-1)))
    return {
        'x_prompt': nrm(ks[0], (BATCH, SEQ, D_MODEL), 1.0),
        'x_sample': nrm(ks[1], (DEC_BATCH, DEC_SEQ, D_MODEL), 1.0),
        'cache_kv0': nrm(ks[2], kv_shape(ATT_GROUPS[0][0]), 1.0),
        'cache_kv1': nrm(ks[3], kv_shape(ATT_GROUPS[1][0]), 1.0),
        'cache_kv2': nrm(ks[4], kv_shape(ATT_GROUPS[2][0]), 1.0),
        'state_ssm': nrm(ks[5], (DEPTH, DEC_BATCH, SSM_HEADS, SSM_HEAD_DIM, SSM_STATE), 0.1),
        'state_conv': nrm(ks[6], (DEPTH, DEC_BATCH, SSM_CONV - 1, XBC_WIDTH), 1.0),
        'state_ffn_conv': nrm(ks[7], (DEPTH, DEC_BATCH, FFN_CONV - 1, 2 * D_FF), 1.0),
        'norm_mix': 1.0 + nrm(ks[8], (DEPTH, D_MODEL), 0.02),
        'w_in': nrm(ks[9], (DEPTH, D_MODEL, IN_WIDTH), D_MODEL ** -0.5),
        'q_norm': 1.0 + nrm(ks[10], (DEPTH, HEAD_DIM), 0.02),
        'k_norm': 1.0 + nrm(ks[11], (DEPTH, HEAD_DIM), 0.02),
        'conv_w': nrm(ks[12], (DEPTH, SSM_CONV, XBC_WIDTH), SSM_CONV ** -0.5),
        'conv_b': nrm(ks[13], (DEPTH, XBC_WIDTH), 0.02),
        'dt_bias': dt0 + jnp.log(-jnp.expm1(-dt0)),
        'a_log': jnp.log(jax.random.uniform(ks[15], (DEPTH, SSM_HEADS), f32, 1.0, 16.0)),
        'd_skip': 1.0 + nrm(ks[16], (DEPTH, SSM_HEADS), 0.1),
        'ssm_norm': 1.0 + nrm(ks[17], (DEPTH, SSM_WIDTH), 0.02),
        'w_out': nrm(ks[18], (DEPTH, MIX_WIDTH, D_MODEL), MIX_WIDTH ** -0.5),
        'norm_ffn': 1.0 + nrm(ks[19], (DEPTH, D_MODEL), 0.02),
        'w_up': nrm(ks[20], (DEPTH, D_MODEL, 2 * D_FF), D_MODEL ** -0.5),
        'ffn_conv_w': nrm(ks[21], (DEPTH, FFN_CONV, 2 * D_FF), FFN_CONV ** -0.5),
        'ffn_conv_b': nrm(ks[22], (DEPTH, 2 * D_FF), 0.02),
        'w_down': nrm(ks[23], (DEPTH, D_FF, D_MODEL), D_FF ** -0.5),
    }


def reference(x_prompt, x_sample, cache_kv0, cache_kv1, cache_kv2, state_ssm, state_conv, state_ffn_conv,
              norm_mix, w_in, q_norm, k_norm, conv_w, conv_b, dt_bias, a_log, d_skip, ssm_norm, w_out,
              norm_ffn, w_up, ffn_conv_w, ffn_conv_b, w_down):
    stacked = (norm_mix, w_in, q_norm, k_norm, conv_w, conv_b, dt_bias, a_log, d_skip, ssm_norm, w_out,
               norm_ffn, w_up, ffn_conv_w, ffn_conv_b, w_down)
    y_prompt, y_sample = x_prompt, x_sample
    nbp = x_prompt.shape[0]
    kv_p, kv_s = ([], [], []), ([], [], [])
    ssm_p, conv_p, ffn_p, ssm_s, conv_s, ffn_s = [], [], [], [], [], []
    for layer in range(DEPTH):
        lw = tuple(w[layer] for w in stacked)
        h0 = jnp.zeros((nbp, SSM_HEADS, SSM_HEAD_DIM, SSM_STATE), jnp.float32)
        conv0 = jnp.zeros((nbp, SSM_CONV - 1, XBC_WIDTH), x_prompt.dtype)
        ffn0 = jnp.zeros((nbp, FFN_CONV - 1, 2 * D_FF), x_prompt.dtype)
        y_prompt, kv, h, c, f = hybrid_layer(y_prompt, lw, None, h0, conv0, ffn0)
        for g in range(len(ATT_GROUPS)):
            kv_p[g].append(kv[g])
        ssm_p.append(h)
        conv_p.append(c)
        ffn_p.append(f)
        y_sample, kv, h, c, f = hybrid_layer(
            y_sample, lw, (cache_kv0[layer], cache_kv1[layer], cache_kv2[layer]),
            state_ssm[layer].astype(jnp.float32), state_conv[layer], state_ffn_conv[layer])
        for g in range(len(ATT_GROUPS)):
            kv_s[g].append(kv[g])
        ssm_s.append(h)
        conv_s.append(c)
        ffn_s.append(f)
    new_kv0_prompt = jnp.stack(kv_p[0])
    new_kv1_prompt = jnp.stack(kv_p[1])
    new_kv2_prompt = jnp.stack(kv_p[2])
    new_ssm_prompt = jnp.stack(ssm_p)
    new_conv_prompt = jnp.stack(conv_p)
    new_ffn_conv_prompt = jnp.stack(ffn_p)
    new_kv0_sample = jnp.stack(kv_s[0])
    new_kv1_sample = jnp.stack(kv_s[1])
    new_kv2_sample = jnp.stack(kv_s[2])
    new_ssm_sample = jnp.stack(ssm_s)
    new_conv_sample = jnp.stack(conv_s)
    new_ffn_conv_sample = jnp.stack(ffn_s)
    return (y_prompt, y_sample,
            new_kv0_prompt, new_kv1_prompt, new_kv2_prompt, new_ssm_prompt, new_conv_prompt, new_ffn_conv_prompt,
            new_kv0_sample, new_kv1_sample, new_kv2_sample, new_ssm_sample, new_conv_sample, new_ffn_conv_sample)
```

```python
import numpy as np
import concourse.bass as bass
import concourse.mybir as mybir
from concourse.bass_utils import run_bass_kernel_spmd

F32 = mybir.dt.float32
BF16 = mybir.dt.bfloat16
AF = mybir.ActivationFunctionType
ALU = mybir.AluOpType
AX = mybir.AxisListType

NCORES = 8
D = 1024
SEQ = 4096
T = 512
NT = SEQ // T
NB_S = 16
TS = 64
IN_W = 4880
DFF = 2816
MIXW = 1792
EPS = 1e-6
PAGE = 512


class Buf:
    __slots__ = ("name", "w", "r")

    def __init__(self, name=""):
        self.name = name
        self.w = None
        self.r = []


class Sched:
    CE = ("pe", "act", "dve", "pool")
    EP = 16000
    NEP = 6

    def __init__(self, nc, n_dma_sp=24, n_dma_pool=40):
        self.nc = nc
        self.lists = {e: [] for e in ("pe", "act", "dve", "pool", "sp")}
        self.cnt = {e: 0 for e in self.CE}
        self.csem = {e: [nc.alloc_semaphore(name=f"c_{e}_{i}") for i in range(self.NEP)] for e in self.CE}
        self.dsem = {"sp": [nc.alloc_semaphore(name=f"d_sp_{i}") for i in range(n_dma_sp)],
                     "pool": [nc.alloc_semaphore(name=f"d_pool_{i}") for i in range(n_dma_pool)]}
        self.dval = {q: [0] * len(self.dsem[q]) for q in self.dsem}
        self.drr = {q: 0 for q in self.dsem}
        self.known = {e: {} for e in self.lists}
        self.pg = {}
        self.psrr = 0
        self.held = set()
        self.psum = [nc.alloc_psum_tensor(f"ps{i}", [128, 512], F32) for i in range(8)]

    def _pages(self, ap):
        name = ap.tensor.name
        if name.startswith("ps"):
            tbl = self.pg.setdefault(name, {})
            b = tbl.get(0)
            if b is None:
                b = tbl[0] = Buf(name)
            return [b]
        esz = mybir.dt.size(ap.dtype)
        dims = ap.ap
        row = dims[0][0]
        lo = ap.offset % row if row > 0 else ap.offset
        hi = lo + 1
        for s, c in dims[1:]:
            hi += (c - 1) * abs(s)
        lo_b = lo * esz
        hi_b = hi * esz
        tbl = self.pg.setdefault(name, {})
        res = []
        for p in range(lo_b // PAGE, (hi_b - 1) // PAGE + 1):
            b = tbl.get(p)
            if b is None:
                b = tbl[p] = Buf(f"{name}:{p}")
            res.append(b)
        return res

    def _bufs(self, items):
        out = []
        for it in items:
            if it is None or isinstance(it, (int, float)):
                continue
            if isinstance(it, Buf):
                out.append(it)
            elif isinstance(it, (list, tuple)):
                out.extend(self._bufs(it))
            else:
                out.extend(self._pages(it))
        return out

    def _need(self, eng, tok, waits):
        if tok is None:
            return
        if tok[0] == "c":
            if eng == "pe" and tok[1] == "pe":
                return
            key = ("c", tok[1])
            v = tok[2]
        else:
            key = ("d", tok[1], tok[2])
            v = tok[3]
        if self.known[eng].get(key, 0) >= v:
            return
        if waits.get(key, 0) < v:
            waits[key] = v

    def _collect(self, eng, reads, writes, extra=()):
        waits = {}
        for b in reads:
            self._need(eng, b.w, waits)
        for b in writes:
            self._need(eng, b.w, waits)
            for t in b.r:
                self._need(eng, t, waits)
        for t in extra:
            self._need(eng, t, waits)
        wl = []
        for key, v in waits.items():
            self.known[eng][key] = v
            if key[0] == "c":
                e = key[1]
                ep = (v - 1) // self.EP
                wl.append((self.csem[e][ep], v - ep * self.EP))
            else:
                wl.append((self.dsem[key[1]][key[2]], v))
        return wl

    def _commit(self, tok, reads, writes):
        for b in reads:
            if len(b.r) > 24:
                d = {}
                for t in b.r:
                    k = t[:2] if t[0] == "c" else t[:3]
                    if k not in d or d[k][-1] < t[-1]:
                        d[k] = t
                b.r = list(d.values())
            b.r.append(tok)
        for b in writes:
            b.w = tok
            b.r = []

    def op(self, eng, fn, reads=(), writes=()):
        reads = self._bufs(reads)
        writes = self._bufs(writes)
        wl = self._collect(eng, reads, writes)
        self.cnt[eng] += 1
        n = self.cnt[eng]
        ep = (n - 1) // self.EP
        assert ep < self.NEP, "too many instructions on " + eng
        self.lists[eng].append((wl, fn, (self.csem[eng][ep], 1)))
        self._commit(("c", eng, n), reads, writes)

    def dma(self, q, out_ap, in_ap, reads=(), writes=(), **kw):
        reads = self._bufs(reads)
        writes = self._bufs(writes)
        if out_ap.tensor.name in ("arena",):
            writes = writes + self._pages(out_ap)
        if in_ap.tensor.name in ("arena",):
            reads = reads + self._pages(in_ap)
        idx = self.drr[q]
        self.drr[q] = (idx + 1) % len(self.dsem[q])
        prev = self.dval[q][idx]
        extra = [("d", q, idx, prev)] if prev > 0 else []
        wl = self._collect(q, reads, writes, extra)
        tgt = prev + 16
        self.dval[q][idx] = tgt

        def fn(eng, out_ap=out_ap, in_ap=in_ap, kw=kw):
            return eng.dma_start(out=out_ap, in_=in_ap, **kw)
        self.lists[q].append((wl, fn, (self.dsem[q][idx], 16)))
        self._commit(("d", q, idx, tgt), reads, writes)

    def finish(self):
        for q in self.dsem:
            wl = []
            for idx, v in enumerate(self.dval[q]):
                if v > 0:
                    wl.append((self.dsem[q][idx], v))
            self.lists[q].append((wl, None, None))

    def emit(self):
        nc = self.nc
        lists = self.lists

        def run(eng, items):
            for wl, fn, inc in items:
                for sem, v in wl:
                    eng.wait_ge(sem, v)
                if fn is not None:
                    ins = fn(eng)
                    ins.then_inc(inc[0], inc[1])

        with nc.Block() as block:
            @block.tensor
            def _(e):
                run(e, lists["pe"])

            @block.scalar
            def _(e):
                run(e, lists["act"])

            @block.vector
            def _(e):
                run(e, lists["dve"])

            @block.gpsimd
            def _(e):
                run(e, lists["pool"])

            @block.sync
            def _(e):
                run(e, lists["sp"])

    def ps(self):
        while True:
            i = self.psrr
            self.psrr = (i + 1) % 8
            if i not in self.held:
                return self.psum[i]

    def ps_hold(self, n):
        out = []
        for _ in range(n):
            t = self.ps()
            self.held.add(int(t.name[2:]))
            out.append(t)
        return out

    def ps_release(self, ts):
        for t in ts:
            self.held.discard(int(t.name[2:]))

    def mm(self, out, pairs, start=True, stop=True):
        reads = [x for p in pairs for x in p]

        def fn(e, out=out, pairs=pairs, start=start, stop=stop):
            n = len(pairs)
            ins = None
            for i, (l, r) in enumerate(pairs):
                ins = e.matmul(out, lhsT=l, rhs=r, start=(start and i == 0), stop=(stop and i == n - 1))
            return ins
        self.op("pe", fn, reads, [out])

    def tr(self, outs_ins, ident):
        reads = [i for _, i in outs_ins] + [ident]
        writes = [o for o, _ in outs_ins]

        def fn(e, oi=outs_ins, ident=ident):
            ins = None
            for o, i in oi:
                ins = e.transpose(o, i, ident)
            return ins
        self.op("pe", fn, reads, writes)

    def act(self, out, in_, func, bias=None, scale=None, accum=None):
        kw = {}
        if bias is not None:
            kw["bias"] = bias
        if scale is not None:
            kw["scale"] = scale
        if accum is not None:
            kw["accum_out"] = accum

        def fn(e, out=out, in_=in_, func=func, kw=kw):
            return e.activation(out=out, in_=in_, func=func, **kw)
        self.op("act", fn, [in_, bias, scale], [out, accum])

    def tt(self, eng, out, a, b, op):
        def fn(e, out=out, a=a, b=b, op=op):
            return e.tensor_tensor(out=out, in0=a, in1=b, op=op)
        self.op(eng, fn, [a, b], [out])

    def stt(self, eng, out, a, scalar, b, op0, op1):
        def fn(e, out=out, a=a, scalar=scalar, b=b, op0=op0, op1=op1):
            return e.scalar_tensor_tensor(out=out, in0=a, scalar=scalar, in1=b, op0=op0, op1=op1)
        self.op(eng, fn, [a, b, scalar], [out])

    def ts(self, eng, out, a, s1, s2, op0, op1=None):
        def fn(e, out=out, a=a, s1=s1, s2=s2, op0=op0, op1=op1):
            if op1 is None:
                return e.tensor_single_scalar(out=out, in_=a, scalar=s1, op=op0)
            return e.tensor_scalar(out=out, in0=a, scalar1=s1, scalar2=s2, op0=op0, op1=op1)
        self.op(eng, fn, [a, s1, s2], [out])

    def cp(self, eng, out, in_):
        if eng == "act":
            return self.act(out, in_, AF.Copy)

        def fn(e, out=out, in_=in_):
            return e.tensor_copy(out=out, in_=in_)
        self.op(eng, fn, [in_], [out])

    def recip(self, out, in_):
        def fn(e, out=out, in_=in_):
            return e.reciprocal(out=out, in_=in_)
        self.op("dve", fn, [in_], [out])

    def memset(self, eng, out, val):
        def fn(e, out=out, val=val):
            return e.memset(out, val)
        self.op(eng, fn, [], [out])

    def red(self, eng, out, in_, op=ALU.add):
        def fn(e, out=out, in_=in_, op=op):
            return e.tensor_reduce(out=out, in_=in_, axis=AX.X, op=op)
        self.op(eng, fn, [in_], [out])


class Arena:
    def __init__(self, nc, nbytes):
        self.nbytes = nbytes
        self.t = nc.alloc_sbuf_tensor("arena", [128, nbytes // 4], F32)
        self.tb = self.t.bitcast(BF16)
        self.rowf = nbytes // 4
        self.rowb = nbytes // 2
        self.cur = 0
        self.hi = 0

    def alloc(self, nbytes):
        off = self.cur
        self.cur = (off + nbytes + PAGE - 1) // PAGE * PAGE
        self.hi = max(self.hi, self.cur)
        assert self.cur <= self.nbytes, (self.cur, self.nbytes)
        return off

    def f(self, off, dims, p0=0, npart=128):
        assert off % 4 == 0
        return bass.AP(self.t, p0 * self.rowf + off // 4, [[self.rowf, npart]] + [list(d) for d in dims])

    def b(self, off, dims, p0=0, npart=128):
        assert off % 2 == 0
        return bass.AP(self.tb, p0 * self.rowb + off // 2, [[self.rowb, npart]] + [list(d) for d in dims])


def psf(ps, col, dims, p0=0, npart=128):
    return bass.AP(ps, p0 * 512 + col, [[512, npart]] + [list(d) for d in dims])


def psb(ps, col, dims, p0=0, npart=128):
    return bass.AP(ps.bitcast(BF16), p0 * 1024 + col, [[1024, npart]] + [list(d) for d in dims])


def dram(t, off, dims):
    return bass.AP(t, off, [list(d) for d in dims])


def _slopes():
    h = np.arange(1, 13, dtype=np.float64)
    return np.exp2(-8.0 * h / 12.0)


def host_consts():
    sl = _slopes()
    p = np.arange(128)[:, None]
    f = np.arange(128)[None, :]
    masks = np.zeros((9, 128, 4, 128), np.float64)
    for g, d in ((0, 1), (1, 4)):
        for a in range(4):
            s = sl[4 * g + a] * d
            m0 = np.where(p <= f, np.exp(-s * (f - p)), 0.0)
            m1 = np.where(f <= p, np.exp(-s * (128 + f - p)), 0.0)
            masks[2 * g + 0, :, a, :] = m0
            masks[2 * g + 1, :, a, :] = m1
    rp, pp = p % 4, p // 4
    rf, pf = f % 4, f // 4
    for rel in range(5):
        dist = 32 * rel + pf - pp
        if rel == 0:
            ok = pp <= pf
        elif rel == 4:
            ok = pf <= pp
        else:
            ok = np.ones_like(dist, bool)
        ok = ok & (rp == rf)
        for a in range(4):
            s = sl[8 + a] * 16
            masks[4 + rel, :, a, :] = np.where(ok, np.exp(-s * dist), 0.0)
    masks = masks.reshape(9, 128, 512).astype(np.float32)
    ident = np.eye(128)
    tri = (p <= f).astype(np.float64)
    slow = (p > f).astype(np.float64)
    ones = np.ones((128, 128))
    cf = np.concatenate([ident, tri, slow, ones], axis=1).astype(np.float32)
    blk = np.zeros((128, 128))
    blk[:64, :64] = 1.0 / 64
    blk[64:, 64:] = 1.0 / 64
    cb = np.concatenate([ident, ones / 1024.0, blk, ones], axis=1).astype(np.float32)
    j = np.arange(64)
    tj, bj = j // 16, j % 16
    same = bj[:, None] == bj[None, :]
    btri = (same & (tj[:, None] <= tj[None, :])).astype(np.float64)
    bslow = (same & (tj[:, None] > tj[None, :])).astype(np.float64)
    rowmask = (bj[:, None] == np.arange(16)[None, :]).astype(np.float64)
    sf = np.zeros((128, 256))
    sf[:64, 0:64] = btri
    sf[:64, 64:128] = bslow
    sf[:64, 128:144] = rowmask
    colmask = np.zeros((128, 16, 64))
    for b in range(16):
        colmask[:, b, :] = (bj == b)[None, :]
    r = np.arange(128)
    maskS = np.zeros((128, 9, 4, 4))
    for a in range(4):
        for t in range(4):
            maskS[:, 0, a, t] = np.where(r >= t, np.exp(-sl[a] * (128 + t - r)), 0.0)
            maskS[:, 1 + t, a, t] = np.exp(-sl[4 + a] * 4 * (128 - r))
            maskS[:, 5 + t, a, t] = np.exp(-sl[8 + a] * 16 * (128 - r))
    mnew = np.zeros((128, 12, 64))
    for h in range(12):
        g = h // 4
        dt_ = tj[None, :] - tj[:, None]
        if g == 0:
            m = np.where(same & (dt_ >= 0), np.exp(-sl[h] * dt_), 0.0)
        else:
            m = (same & (dt_ == 0)).astype(np.float64)
        mnew[:64, h, :] = m
    sb = np.concatenate([colmask.reshape(128, 1024), maskS.reshape(128, 144), mnew.reshape(128, 768)], axis=1)
    return {"cmask": masks, "cst_f": cf, "cst_b": cb, "cst_sf": sf.astype(np.float32), "cst_sb": sb.astype(np.float32)}


WNAMES = ["norm_mix", "w_in", "q_norm", "k_norm", "conv_w", "conv_b", "dt_bias", "a_log", "d_skip",
          "ssm_norm", "w_out", "norm_ffn", "w_up", "ffn_conv_w", "ffn_conv_b", "w_down"]
WSHAPES = {"norm_mix": [2, 1024], "w_in": [2, 1024, IN_W], "q_norm": [2, 64], "k_norm": [2, 64],
           "conv_w": [2, 4, 1536], "conv_b": [2, 1536], "dt_bias": [2, 16], "a_log": [2, 16],
           "d_skip": [2, 16], "ssm_norm": [2, 1024], "w_out": [2, MIXW, 1024], "norm_ffn": [2, 1024],
           "w_up": [2, 1024, 2 * DFF], "ffn_conv_w": [2, 3, 2 * DFF], "ffn_conv_b": [2, 2 * DFF],
           "w_down": [2, DFF, 1024]}


def build(cfg):
    nlayers = cfg.get("nlayers", 2)
    ntiles = cfg.get("ntiles", NT)
    do_sample = cfg.get("sample", True)
    nc = bass.Bass("TRN2", target_bir_lowering=False)
    S = Sched(nc)
    A = Arena(nc, 207 * 1024)

    def din(name, shape):
        return nc.dram_tensor(name, shape, F32, kind="ExternalInput")

    def dout(name, shape):
        return nc.dram_tensor(name, shape, F32, kind="ExternalOutput")

    x_p = din("x_p", [SEQ, D])
    x_s = din("x_s", [TS, D])
    ckv = [din("ckv0", [2, NB_S, 128, 512]), din("ckv1", [2, NB_S, 512, 512]), din("ckv2", [2, NB_S, 2048, 512])]
    sssm = din("sssm", [2, NB_S, 16, 64, 128])
    sconv = din("sconv", [2, NB_S, 3, 1536])
    sffn = din("sffn", [2, NB_S, 2, 2 * DFF])
    W = {n: din(n, WSHAPES[n]) for n in WNAMES}
    cmask_d = din("cmask", [9, 128, 512])
    cstf_d = din("cst_f", [128, 512])
    cstb_d = din("cst_b", [128, 512])
    cstsf_d = din("cst_sf", [128, 256])
    cstsb_d = din("cst_sb", [128, 1936])

    y_p = dout("y_p", [SEQ, D])
    y_s = dout("y_s", [TS, D])
    kvp = [dout("kv0_p", [2, 128, 512]), dout("kv1_p", [2, 512, 512]), dout("kv2_p", [2, 2048, 512])]
    ssm_p = dout("ssm_p", [2, 1024, 128])
    conv_p = dout("conv_p", [2, 3, 1536])
    ffn_p = dout("ffn_p", [2, 2, 2 * DFF])
    kvs = [dout(f"kv{g}_s", [2, TS, 512]) for g in range(3)]
    ssm_s = dout("ssm_s", [2, NB_S, 1024, 128])
    conv_s = dout("conv_s", [2, NB_S, 3, 1536])
    ffn_s = dout("ffn_s", [2, NB_S, 2, 2 * DFF])

    wb_in = nc.dram_tensor("wb_in", [2, 1024, IN_W], BF16)
    wb_out = nc.dram_tensor("wb_out", [2, MIXW, 1024], BF16)
    wb_up = nc.dram_tensor("wb_up", [2, 1024, 2 * DFF], BF16)
    wb_down = nc.dram_tensor("wb_down", [2, DFF, 1024], BF16)
    x1T = nc.dram_tensor("x1T", [8, 128, SEQ], F32)
    wbufs = {}

    def convert(name, src, dst, l, rows, cols, rb):
        bl = []
        for r0 in range(0, rows, rb):
            r1 = min(rows, r0 + rb)
            b = Buf(f"{name}{l}_{r0}")
            S.dma("pool", dram(dst, (l * rows + r0) * cols, [[cols, r1 - r0], [1, cols]]),
                  dram(src, (l * rows + r0) * cols, [[cols, r1 - r0], [1, cols]]), writes=[b])
            bl.append(b)
        wbufs[(name, l)] = bl

    for l in range(nlayers):
        convert("in", W["w_in"], wb_in, l, 1024, IN_W, 128)
        convert("out", W["w_out"], wb_out, l, MIXW, 1024, 448)
        convert("up", W["w_up"], wb_up, l, 1024, 2 * DFF, 128)
        convert("down", W["w_down"], wb_down, l, DFF, 1024, 704)

    o_cf = A.alloc(512 * 4)
    o_cb = A.alloc(512 * 2)
    o_mask = A.alloc(9 * 1024)
    o_par = A.alloc(2048)
    o_kgbc = A.alloc(256)
    o_DI = A.alloc(4096)
    o_xT = A.alloc(8 * T * 4)
    o_hT = A.alloc(8 * T * 2)
    o_mix = A.alloc(14 * T * 2)
    NRING = 4
    o_wr = [A.alloc(8192) for _ in range(NRING)]
    RK = (2, 2, 5)
    o_kr = [[A.alloc(2 * T * 2) for _ in range(RK[g])] for g in range(3)]
    o_vr = [[A.alloc(4 * 256 * 2) for _ in range(RK[g])] for g in range(3)]
    o_vones = A.alloc(128)
    o_halo = A.alloc(12 * 3 * 4)
    o_fhalo = A.alloc(44 * 2 * 4)
    o_H = A.alloc(4096)
    o_Hb = A.alloc(2048)
    o_sq = [A.alloc(T * 2) for _ in range(2)]
    o_rstd = A.alloc(T * 4)
    o_small = A.alloc(1024)
    o_xTs = A.alloc(8 * TS * 4)
    o_ssf = A.alloc(256 * 4)
    o_ssb = A.alloc(1936 * 2)
    base = A.cur

    def cf(i):
        return A.f(o_cf + i * 512, [[1, 128]])
    ident_f, tri_f, slow_f, ones_f = cf(0), cf(1), cf(2), cf(3)

    def cb(i):
        return A.b(o_cb + i * 256, [[1, 128]])
    ident_b, onesdiv_b, blk_b, ones_b = cb(0), cb(1), cb(2), cb(3)

    S.dma("sp", A.f(o_cf, [[1, 512]]), cstf_d.ap())
    S.dma("pool", A.b(o_cb, [[1, 512]]), cstb_d.ap())
    for i in range(9):
        S.dma("pool", A.b(o_mask + i * 1024, [[1, 512]]), dram(cmask_d, i * 128 * 512, [[512, 128], [1, 512]]))
    S.memset("dve", A.b(o_vones, [[1, 64]]), 1.0)
    S.dma("sp", A.f(o_ssf, [[1, 256]]), cstsf_d.ap())
    S.dma("pool", A.b(o_ssb, [[1, 1936]]), cstsb_d.ap())

    P_GMIX, P_GFFN, P_GSSM, P_QG, P_KG, P_CW, P_CB, P_FCW, P_FCB, P_DTB, P_A, P_DSK = (
        0, 8, 16, 24, 25, 32, 80, 96, 228, 272, 288, 304)

    def par(eoff, dims):
        return A.f(o_par + 4 * eoff, dims)

    def load_params(l):
        q = "sp"
        ns = dict(allow_slow_non_contiguous=True)
        S.dma(q, par(P_GMIX, [[1, 8]]), dram(W["norm_mix"], l * 1024, [[1, 128], [128, 8]]), **ns)
        S.dma(q, par(P_GFFN, [[1, 8]]), dram(W["norm_ffn"], l * 1024, [[1, 128], [128, 8]]), **ns)
        S.dma(q, par(P_GSSM, [[1, 8]]), dram(W["ssm_norm"], l * 1024, [[1, 128], [128, 8]]), **ns)
        for hf in range(2):
            S.dma(q, A.f(o_par + 4 * P_QG, [[1, 1]], p0=64 * hf, npart=64), dram(W["q_norm"], l * 64, [[1, 64], [1, 1]]), **ns)
            S.dma(q, A.f(o_par + 4 * P_KG, [[1, 1]], p0=64 * hf, npart=64), dram(W["k_norm"], l * 64, [[1, 64], [1, 1]]), **ns)
        for tp in range(4):
            S.dma(q, par(P_CW + tp, [[4, 12]]), dram(W["conv_w"], (l * 4 + tp) * 1536, [[1, 128], [128, 12]]), **ns)
        S.dma(q, par(P_CB, [[1, 12]]), dram(W["conv_b"], l * 1536, [[1, 128], [128, 12]]), **ns)
        for tp in range(3):
            S.dma(q, par(P_FCW + tp, [[3, 44]]), dram(W["ffn_conv_w"], (l * 3 + tp) * 2 * DFF, [[1, 128], [128, 44]]), **ns)
        S.dma(q, par(P_FCB, [[1, 44]]), dram(W["ffn_conv_b"], l * 2 * DFF, [[1, 128], [128, 44]]), **ns)
        S.dma(q, par(P_DTB, [[1, 16]]), dram(W["dt_bias"], l * 16, [[0, 128], [1, 16]]), **ns)
        S.dma(q, par(P_A, [[1, 16]]), dram(W["a_log"], l * 16, [[0, 128], [1, 16]]), **ns)
        S.dma(q, par(P_DSK, [[1, 16]]), dram(W["d_skip"], l * 16, [[0, 128], [1, 16]]), **ns)
        S.dma(q, A.f(o_kgbc, [[1, 64]]), dram(W["k_norm"], l * 64, [[0, 128], [1, 64]]), **ns)
        S.act(par(P_A, [[1, 16]]), par(P_A, [[1, 16]]), AF.Exp)
        S.ts("dve", par(P_A, [[1, 16]]), par(P_A, [[1, 16]]), -1.0, None, ALU.mult)
        S.tt("dve", A.b(o_DI, [[128, 16], [1, 128]]), A.f(o_cf, [[0, 16], [1, 128]]),
             par(P_DSK, [[1, 16], [0, 128]]), ALU.mult)

    ring = {"i": 0}

    def slab(wkey, l, src, row0, nk, ncols_total, col0, ncols, dst_col0=0, slot=None, width=None):
        if slot is None:
            slot = ring["i"]
            ring["i"] = (slot + 1) % NRING
        width = width or ncols
        rows_total = {"in": 1024, "out": MIXW, "up": 1024, "down": DFF}[wkey]
        src_ap = dram(src, (l * rows_total + row0 * 128) * ncols_total + col0,
                      [[ncols_total, 128], [128 * ncols_total, nk], [1, ncols]])
        dst_ap = A.b(o_wr[slot] + 2 * dst_col0, [[width, nk], [1, ncols]])
        S.dma("sp", dst_ap, src_ap, reads=wbufs[(wkey, l)])
        return slot

    def wv(slot, k, col0, ncols, width):
        return A.b(o_wr[slot] + 2 * (k * width + col0), [[1, ncols]])

    def rmsnorm_fm(xoff, Tn, gpar):
        ps = S.ps()
        for c in range(8):
            sq = A.b(o_sq[c % 2], [[1, Tn]])
            S.act(sq, A.f(xoff + 4 * c * Tn, [[1, Tn]]), AF.Square)
            S.mm(psf(ps, 0, [[1, Tn]]), [(onesdiv_b, sq)], start=(c == 0), stop=(c == 7))
        rstd = A.f(o_rstd, [[1, Tn]])
        S.act(rstd, psf(ps, 0, [[1, Tn]]), AF.Sqrt, bias=EPS)
        S.recip(rstd, rstd)
        for c in range(8):
            S.stt("dve", A.b(o_hT + 2 * c * Tn, [[1, Tn]]), A.f(xoff + 4 * c * Tn, [[1, Tn]]),
                  par(gpar + c, [[1, 1]]), rstd, ALU.mult, ALU.mult)

    A.cur = base
    o_xtok = [A.alloc(4096) for _ in range(2)]
    o_gate = A.alloc(4 * 1024 * 2)
    o_raw = [A.alloc(2560) for _ in range(2)]
    o_acc = [A.alloc(2048) for _ in range(2)]
    o_xc = A.alloc(12 * T * 2)
    o_dtp = A.alloc(4 * 16 * 4)
    o_xtk = A.alloc(2048)
    o_xdt = A.alloc(2048)
    o_xde = A.alloc(2048)
    o_btk = A.alloc(512)
    o_R1 = A.alloc(2048)
    o_LT = [A.alloc(2048) for _ in range(2)]
    o_G = [A.alloc(1024) for _ in range(2)]
    o_CBm = A.alloc(1024)
    o_yis = A.alloc(4096)
    o_yn = A.alloc(2048)
    A.cur = base
    o_q = A.alloc(6 * T * 2)
    o_sqb = [A.alloc(T * 2) for _ in range(2)]
    o_rq = [A.alloc(T * 4) for _ in range(2)]
    o_kvst = A.alloc(1536 * 4)
    o_sqk = A.alloc(768 * 4)
    o_ssk = A.alloc(512)
    o_E = [A.alloc(1024) for _ in range(3)]
    o_P = [A.alloc(1024) for _ in range(3)]
    o_U = A.alloc(6 * T * 4)
    o_Dg = A.alloc(3 * 2 * T * 4)
    A.cur = base
    o_uraw = [A.alloc(2560) for _ in range(4)]
    o_facc = [A.alloc(2048) for _ in range(4)]
    o_sg = [A.alloc(2048) for _ in range(2)]
    o_act = A.alloc(22 * T * 2)
    o_ytok = [A.alloc(4096) for _ in range(2)]
    x1buf = {}

    def sm(k, n=16):
        return A.f(o_small + 4 * k, [[1, n]])

    def load_x_prompt(l, i):
        if l == 0:
            for tb in range(4):
                st = o_xtok[tb % 2]
                S.dma("sp", A.f(st, [[1, 1024]]), dram(x_p, (i * T + tb * 128) * D, [[D, 128], [1, D]]))
                for half in range(2):
                    ps = S.ps()
                    S.tr([(psf(ps, j * 128, [[1, 128]]), A.f(st + 4 * (half * 4 + j) * 128, [[1, 128]])) for j in range(4)], ident_f)
                    S.cp("act" if half == 0 else "dve", A.f(o_xT + 4 * (half * 4 * T + tb * 128), [[T, 4], [1, 128]]),
                         psf(ps, 0, [[128, 4], [1, 128]]))
        else:
            S.dma("pool", A.f(o_xT, [[T, 8], [1, T]]), dram(x1T, i * T, [[SEQ, 128], [128 * SEQ, 8], [1, T]]),
                  reads=[x1buf[i]])

    def p1a_proj(l, Tn, ntb, tbw, conv_fn, xbc_tok=None):
        for zs in range(2):
            sl = slab("in", l, wb_in, 0, 8, IN_W, 2304 + 512 * zs, 512)
            for tb in range(ntb):
                ps = S.ps()
                S.mm(psf(ps, 0, [[1, 512]], npart=tbw),
                     [(A.b(o_hT + 2 * (k * Tn + tb * 128), [[1, tbw]]), wv(sl, k, 0, 512, 512)) for k in range(8)])
                S.act(A.b(o_gate + 2 * (tb * 1024 + zs * 512), [[1, 512]], npart=tbw), psf(ps, 0, [[1, 512]], npart=tbw), AF.Silu)
        for xs in range(3):
            sl = slab("in", l, wb_in, 0, 8, IN_W, 3328 + 512 * xs, 512)
            for j in range(4):
                c = xs * 4 + j
                ps = S.ps()
                S.mm(psf(ps, 0, [[1, Tn]]), [(wv(sl, k, j * 128, 128, 512), A.b(o_hT + 2 * k * Tn, [[1, Tn]])) for k in range(8)])
                conv_fn(c, ps)
            if xbc_tok is not None:
                xbc_tok(xs, sl)
        sl = slab("in", l, wb_in, 0, 8, IN_W, 4864, 16)
        for tb in range(ntb):
            ps = S.ps()
            S.mm(psf(ps, 0, [[1, 16]], npart=tbw),
                 [(A.b(o_hT + 2 * (k * Tn + tb * 128), [[1, tbw]]), wv(sl, k, 0, 16, 16)) for k in range(8)])
            d = A.f(o_dtp + 64 * tb, [[1, 16]], npart=tbw)
            S.tt("dve", d, psf(ps, 0, [[1, 16]], npart=tbw), A.f(o_par + 4 * P_DTB, [[1, 16]], npart=tbw), ALU.add)
            S.act(d, d, AF.Exp)
            S.act(d, d, AF.Ln, bias=1.0)

    def conv_prompt(c, ps):
        raw = o_raw[c % 2]
        S.cp("act", A.f(raw + 12, [[1, T]]), psf(ps, 0, [[1, T]]))
        S.cp("act", A.f(raw, [[1, 3]]), A.f(o_halo + 12 * c, [[1, 3]]))
        acc = A.f(o_acc[c % 2], [[1, T]])
        S.ts("dve", acc, A.f(raw + 12, [[1, T]]), par(P_CW + 4 * c + 3, [[1, 1]]), par(P_CB + c, [[1, 1]]), ALU.mult, ALU.add)
        for tap in (2, 1, 0):
            S.stt("dve", acc, A.f(raw + 4 * tap, [[1, T]]), par(P_CW + 4 * c + tap, [[1, 1]]), acc, ALU.mult, ALU.add)
        S.cp("act", A.f(o_halo + 12 * c, [[1, 3]]), A.f(raw + 4 * T, [[1, 3]]))
        S.act(A.b(o_xc + 2 * c * T, [[1, T]]), acc, AF.Silu)

    def ssd_chunk(Tn, tb, Pn, tri_ap, slow_ap, tri_off, hstate=None):
        cols = tb * 128
        dtp = A.f(o_dtp + 64 * tb, [[1, 16]], npart=Pn)
        dta = A.f(o_small, [[1, 16]], npart=Pn)

        def smp(k, n=16):
            return A.f(o_small + 4 * k, [[1, n]], npart=Pn)
        S.tt("dve", dta, dtp, A.f(o_par + 4 * P_A, [[1, 16]], npart=Pn), ALU.mult)
        ps = S.ps()
        S.mm(psf(ps, 0, [[1, 16]], npart=Pn), [(tri_ap, dta)])
        S.act(smp(32), psf(ps, 0, [[1, 16]], npart=Pn), AF.Exp)
        ps = S.ps()
        S.tr([(psb(ps, c * 128, [[1, 128]], npart=Pn), A.b(o_xc + 2 * (c * Tn + cols), [[1, Pn]])) for c in range(8)], ident_b)
        S.cp("act", A.b(o_xtk, [[1, 1024]], npart=Pn), psb(ps, 0, [[1, 1024]], npart=Pn))
        ps = S.ps()
        S.tr([(psb(ps, g * 128, [[1, 128]], npart=Pn), A.b(o_xc + 2 * ((8 + g) * Tn + cols), [[1, Pn]])) for g in range(2)], ident_b)
        S.cp("act", A.b(o_btk, [[1, 256]], npart=Pn), psb(ps, 0, [[1, 256]], npart=Pn))
        S.tt("pool", A.b(o_xdt, [[64, 16], [1, 64]], npart=Pn), A.b(o_xtk, [[64, 16], [1, 64]], npart=Pn),
             A.f(o_dtp + 64 * tb, [[1, 16], [0, 64]], npart=Pn), ALU.mult)
        ps = S.ps()
        for g in range(2):
            S.mm(psf(ps, g * 128, [[1, Pn]], npart=Pn),
                 [(A.b(o_xc + 2 * ((8 + g) * Tn + cols), [[1, Pn]]), A.b(o_xc + 2 * ((10 + g) * Tn + cols), [[1, Pn]]))])
        S.tt("dve", A.f(o_CBm, [[128, 2], [1, Pn]], npart=Pn), psf(ps, 0, [[128, 2], [1, Pn]], npart=Pn),
             A.f(tri_off, [[0, 2], [1, Pn]], npart=Pn), ALU.mult)
        psyi = [S.ps(), S.ps()]
        if hstate is None:
            for g in range(2):
                S.mm(psf(psyi[g], 0, [[1, 512]], npart=Pn),
                     [(A.b(o_xc + 2 * ((10 + g) * Tn + cols), [[1, Pn]]), A.b(o_Hb + 2 * g * 512, [[1, 512]]))])
        else:
            hstate["yinter"](psyi)
        for g in range(2):
            S.tt("dve", A.f(o_yis + 4 * g * 512, [[64, 8], [1, 64]], npart=Pn), psf(psyi[g], 0, [[64, 8], [1, 64]], npart=Pn),
                 A.f(o_small + 4 * (32 + 8 * g), [[1, 8], [0, 64]], npart=Pn), ALU.mult)
        psy = [S.ps(), S.ps()]
        for hg in range(4):
            g = hg // 2
            S.tt("pool", A.f(o_R1, [[128, 4], [1, Pn]], npart=Pn), A.f(tri_off, [[0, 4], [1, Pn]], npart=Pn),
                 A.f(o_small + 4 * (4 * hg), [[1, 4], [0, Pn]], npart=Pn), ALU.mult)
            ps = S.ps()
            S.mm(psf(ps, 0, [[1, 512]], npart=Pn), [(slow_ap, A.f(o_R1, [[1, 512]], npart=Pn))])
            LT = o_LT[hg % 2]
            S.act(A.f(LT, [[1, 512]], npart=Pn), psf(ps, 0, [[1, 512]], npart=Pn), AF.Exp)
            if hstate is None:
                S.cp("act", A.f(o_small + 4 * (64 + 4 * hg), [[1, 4]], npart=Pn), A.f(LT + 4 * (Pn - 1), [[128, 4]], npart=Pn))
            G = o_G[hg % 2]
            S.tt("dve", A.b(G, [[128, 4], [1, Pn]], npart=Pn), A.f(LT, [[128, 4], [1, Pn]], npart=Pn),
                 A.f(o_CBm + 4 * g * 128, [[0, 4], [1, Pn]], npart=Pn), ALU.mult)
            for a in range(4):
                h = 4 * hg + a
                S.mm(psf(psy[h // 8], (h % 8) * 64, [[1, 64]], npart=Pn),
                     [(A.b(G + 2 * a * 128, [[1, Pn]], npart=Pn), A.b(o_xdt + 2 * h * 64, [[1, 64]], npart=Pn)),
                      (A.b(o_DI + 2 * h * 128, [[1, Pn]], npart=Pn), A.b(o_xtk + 2 * h * 64, [[1, 64]], npart=Pn))])
        for g in range(2):
            yv = A.f(o_yis + 4 * g * 512, [[1, 512]], npart=Pn)
            S.tt("dve", yv, psf(psy[g], 0, [[1, 512]], npart=Pn), yv, ALU.add)
        yall = A.f(o_yis, [[1, 1024]], npart=Pn)
        S.tt("dve", yall, yall, A.b(o_gate + 2 * tb * 1024, [[1, 1024]], npart=Pn), ALU.mult)
        S.act(A.b(o_yn, [[1, 1024]], npart=Pn), yall, AF.Square, accum=smp(96, 1))
        S.act(smp(97, 1), smp(96, 1), AF.Sqrt, bias=EPS, scale=1.0 / 1024)
        S.recip(smp(97, 1), smp(97, 1))
        S.ts("dve", A.b(o_yn, [[1, 1024]], npart=Pn), yall, smp(97, 1), None, ALU.mult)
        ps = S.ps()
        S.tr([(psb(ps, c * 128, [[1, Pn]]), A.b(o_yn + 2 * c * 128, [[1, 128]], npart=Pn)) for c in range(8)], A.b(o_cb, [[1, Pn]], npart=Pn))
        S.tt("dve", A.b(o_mix + 2 * (6 * Tn + cols), [[Tn, 8], [1, Pn]]), psb(ps, 0, [[128, 8], [1, Pn]]),
             par(P_GSSM, [[1, 8], [0, Pn]]), ALU.mult)
        if hstate is None:
            ps = S.ps()
            S.mm(psf(ps, 16, [[1, 16]]), [(ones_f, dta)])
            S.act(smp(48), psf(ps, 16, [[1, 16]]), AF.Exp)
            S.tt("pool", smp(80), dtp, smp(64), ALU.mult)
            S.tt("pool", A.b(o_xde, [[64, 16], [1, 64]]), A.b(o_xtk, [[64, 16], [1, 64]]),
                 A.f(o_small + 4 * 80, [[1, 16], [0, 64]]), ALU.mult)
            psS = [S.ps(), S.ps()]
            for g in range(2):
                S.mm(psf(psS[g], 0, [[1, 512]]), [(A.b(o_btk + 2 * g * 128, [[1, 128]]), A.b(o_xde + 2 * g * 512, [[1, 512]]))])
            Hv = A.f(o_H, [[64, 16], [1, 64]])
            S.tt("dve", Hv, Hv, A.f(o_small + 4 * 48, [[1, 16], [0, 64]]), ALU.mult)
            for g in range(2):
                hv = A.f(o_H + 4 * g * 512, [[1, 512]])
                S.tt("dve", hv, hv, psf(psS[g], 0, [[1, 512]]), ALU.add)
            S.cp("act", A.b(o_Hb, [[1, 1024]]), A.f(o_H, [[1, 1024]]))
        else:
            hstate["update"](dta, dtp)
    def mi(g, rel):
        return (rel, 2 + rel, 4 + rel)[g]

    def qk_chunk(sl, j, is_k, cc, i, Tn, permute):
        ps = S.ps()
        S.mm(psf(ps, 0, [[1, Tn]]), [(wv(sl, k, j * 128, 128, 512), A.b(o_hT + 2 * k * Tn, [[1, Tn]])) for k in range(8)])
        sqb = A.b(o_sqb[cc % 2], [[1, Tn]])
        S.act(sqb, psf(ps, 0, [[1, Tn]]), AF.Square)
        ps2 = S.ps()
        S.mm(psf(ps2, 0, [[1, Tn]]), [(blk_b, sqb)])
        rq = A.f(o_rq[cc % 2], [[1, Tn]])
        S.act(rq, psf(ps2, 0, [[1, Tn]]), AF.Sqrt, bias=EPS)
        S.recip(rq, rq)
        g = cc // 2
        if is_k:
            dst = o_kr[g][i % RK[g]] + 2 * (cc % 2) * Tn if permute else o_kr[0][0] + 2 * cc * Tn
        else:
            dst = o_q + 2 * cc * Tn
        gp = par(P_KG if is_k else P_QG, [[1, 1]])
        if g == 0 or not permute:
            S.stt("dve", A.b(dst, [[1, Tn]]), psf(ps, 0, [[1, Tn]]), gp, rq, ALU.mult, ALU.mult)
        else:
            S.stt("dve", A.b(dst, [[1, 128], [128, 4]]), psf(ps, 0, [[4, 128], [1, 4]]), gp,
                  A.f(o_rq[cc % 2], [[4, 128], [1, 4]]), ALU.mult, ALU.mult)

    def kv_natural(l, Tn, ntb, tbw, out_fn):
        sls = [slab("in", l, wb_in, 0, 8, IN_W, 768 + 512 * n, 512) for n in range(3)]
        for tb in range(ntb):
            for n in range(3):
                ps = S.ps()
                S.mm(psf(ps, 0, [[1, 512]], npart=tbw),
                     [(A.b(o_hT + 2 * (k * Tn + tb * 128), [[1, tbw]]), wv(sls[n], k, 0, 512, 512)) for k in range(8)])
                S.cp("act", A.f(o_kvst + 4 * n * 512, [[1, 512]], npart=tbw), psf(ps, 0, [[1, 512]], npart=tbw))
            kv = A.f(o_kvst, [[1, 768]], npart=tbw)
            kv3 = A.f(o_kvst, [[64, 12], [1, 64]], npart=tbw)
            S.tt("pool", A.f(o_sqk, [[1, 768]], npart=tbw), kv, kv, ALU.mult)
            ssk = A.f(o_ssk, [[1, 12]], npart=tbw)
            S.red("dve", ssk, A.f(o_sqk, [[64, 12], [1, 64]], npart=tbw))
            S.act(ssk, ssk, AF.Sqrt, bias=EPS, scale=1.0 / 64)
            S.recip(ssk, ssk)
            S.tt("dve", kv3, kv3, A.f(o_ssk, [[1, 12], [0, 64]], npart=tbw), ALU.mult)
            S.tt("dve", kv3, kv3, A.f(o_kgbc, [[0, 12], [1, 64]], npart=tbw), ALU.mult)
            out_fn(tb)

    def p1b_prompt(l, i):
        qk_slabs = [(0, [(False, 0), (False, 1), (False, 2), (False, 3)]),
                    (512, [(False, 4), (False, 5), (True, 0), (True, 1)]),
                    (1024, [(True, 2), (True, 3), (True, 4), (True, 5)])]
        for col0, items in qk_slabs:
            sl = slab("in", l, wb_in, 0, 8, IN_W, col0, 512)
            for j, (is_k, cc) in enumerate(items):
                qk_chunk(sl, j, is_k, cc, i, T, True)
        lvl = cfg.get("p1b", 99)
        if lvl <= 1:
            return
        slv0 = slab("in", l, wb_in, 0, 8, IN_W, 1536, 512)
        slv1 = slab("in", l, wb_in, 0, 8, IN_W, 2048, 256)
        for g in range(3):
            sl, c0, width = ((slv0, 0, 512), (slv0, 256, 512), (slv1, 0, 256))[g]
            for u in range(4):
                ps = S.ps()
                if g == 0:
                    lhs = [A.b(o_hT + 2 * (k * T + u * 128), [[1, 128]]) for k in range(8)]
                else:
                    lhs = [A.b(o_hT + 2 * (k * T + u), [[4, 128]]) for k in range(8)]
                S.mm(psf(ps, 0, [[1, 256]]), [(lhs[k], wv(sl, k, c0, 256, width)) for k in range(8)])
                S.cp("act", A.b(o_vr[g][i % RK[g]] + 2 * u * 256, [[1, 256]]), psf(ps, 0, [[1, 256]]))
        if i >= NT - 4:
            def out_fn(tb):
                tok = i * T + tb * 128
                for g, Wg in enumerate((128, 512, 2048)):
                    r0 = tok - (SEQ - Wg)
                    if r0 >= 0:
                        S.dma("pool", dram(kvp[g], (l * Wg + r0) * 512, [[512, 128], [256, 2], [1, 256]]),
                              A.f(o_kvst + 4 * g * 256, [[768, 2], [1, 256]]))
            kv_natural(l, T, 4, 128, out_fn)
        if lvl <= 2:
            return
        for g in range(3):
            if lvl <= 3 + g:
                return
            cur = i % RK[g]
            psU = S.ps_hold(4)
            for u in range(4):
                if g == 0:
                    kcs = []
                    if u > 0:
                        kcs.append((cur, u - 1, 1))
                    elif i > 0:
                        kcs.append(((i - 1) % RK[0], 3, 1))
                    kcs.append((cur, u, 0))
                elif g == 1:
                    kcs = ([((i - 1) % RK[1], u, 1)] if i > 0 else []) + [(cur, u, 0)]
                else:
                    kcs = [((i - di) % RK[2], u, di) for di in range(4, -1, -1) if i - di >= 0]
                psD = S.ps_hold(1)[0]
                for n, (slot, ku, rel) in enumerate(kcs):
                    psS2 = [S.ps(), S.ps()]
                    for a in range(4):
                        cc2, pb = a // 2, 64 * (a % 2)
                        S.mm(psf(psS2[a % 2], cc2 * 128, [[1, 128]]),
                             [(A.b(o_kr[g][slot] + 2 * (cc2 * T + ku * 128), [[1, 128]], p0=pb, npart=64),
                               A.b(o_q + 2 * ((2 * g + cc2) * T + u * 128), [[1, 128]], p0=pb, npart=64))])
                    E = A.b(o_E[n % 3], [[1, 512]])
                    for hh in range(2):
                        S.act(A.b(o_E[n % 3] + 2 * hh * 128, [[256, 2], [1, 128]]), psf(psS2[hh], 0, [[128, 2], [1, 128]]),
                              AF.Exp, scale=0.125)
                    Pm = A.b(o_P[n % 3], [[1, 512]])
                    S.tt("pool", Pm, E, A.b(o_mask + 1024 * mi(g, rel), [[1, 512]]), ALU.mult)
                    first, last = (n == 0), (n == len(kcs) - 1)
                    for a in range(4):
                        S.mm(psf(psU[a], u * 128, [[1, 128]], npart=64),
                             [(A.b(o_vr[g][slot] + 2 * (ku * 256 + a * 64), [[1, 64]]), A.b(o_P[n % 3] + 2 * a * 128, [[1, 128]]))],
                             start=first, stop=last)
                    S.mm(psf(psD, 0, [[1, 512]], npart=64), [(A.b(o_vones, [[1, 64]]), Pm)], start=first, stop=last)
                for par_ in range(2):
                    src = psf(psD, par_ * 128, [[256, 2], [1, 128]], npart=64)
                    if g == 0:
                        dst = A.f(o_Dg + 4 * (g * 2 * T + u * 128), [[T, 2], [1, 128]], p0=64 * par_, npart=64)
                    else:
                        dst = A.f(o_Dg + 4 * (g * 2 * T + u), [[T, 2], [4, 128]], p0=64 * par_, npart=64)
                    S.cp("act" if par_ else "dve", dst, src)
                S.ps_release([psD])
            for a in range(4):
                cc, pb = 2 * g + a // 2, 64 * (a % 2)
                if g == 0:
                    S.cp("act", A.f(o_U + 4 * cc * T, [[1, T]], p0=pb, npart=64), psf(psU[a], 0, [[1, T]], npart=64))
                else:
                    S.cp("act", A.f(o_U + 4 * cc * T, [[1, 4], [4, 128]], p0=pb, npart=64), psf(psU[a], 0, [[128, 4], [1, 128]], npart=64))
            S.ps_release(psU)
        att_finish(T)

    def att_finish(Tn):
        def Dall(g):
            return A.f(o_Dg + 4 * g * 2 * Tn, [[1, 2 * Tn]])
        S.tt("dve", Dall(0), Dall(0), Dall(1), ALU.add)
        S.tt("dve", Dall(0), Dall(0), Dall(2), ALU.add)
        S.recip(Dall(0), Dall(0))
        for cc in range(6):
            S.tt("pool" if cc % 2 else "dve", A.b(o_mix + 2 * cc * Tn, [[1, Tn]]), A.f(o_U + 4 * cc * Tn, [[1, Tn]]),
                 A.f(o_Dg + 4 * (cc % 2) * Tn, [[1, Tn]]), ALU.mult)

    def p2(l, Tn, xoff, conv_fn, up_tok=None):
        for ms in range(4):
            sl = slab("out", l, wb_out, 0, 14, 1024, ms * 256, 256)
            for j in range(2):
                m = ms * 2 + j
                ps = S.ps()
                S.mm(psf(ps, 0, [[1, Tn]]), [(wv(sl, k, j * 128, 128, 256), A.b(o_mix + 2 * k * Tn, [[1, Tn]])) for k in range(14)])
                xv = A.f(xoff + 4 * m * Tn, [[1, Tn]])
                S.tt("dve", xv, xv, psf(ps, 0, [[1, Tn]]), ALU.add)
        rmsnorm_fm(xoff, Tn, P_GFFN)
        for js in range(11):
            slot = slab("up", l, wb_up, 0, 8, 2 * DFF, 256 * js, 256, dst_col0=0, width=512)
            slab("up", l, wb_up, 0, 8, 2 * DFF, DFF + 256 * js, 256, dst_col0=256, slot=slot, width=512)
            for a in range(2):
                m = 2 * js + a
                accs = []
                for which, cbase, ch in ((0, a * 128, m), (1, 256 + a * 128, 22 + m)):
                    ps = S.ps()
                    S.mm(psf(ps, 0, [[1, Tn]]), [(wv(slot, k, cbase, 128, 512), A.b(o_hT + 2 * k * Tn, [[1, Tn]])) for k in range(8)])
                    accs.append(conv_fn(ch, ps, (2 * m + which) % 4))
                sg = A.f(o_sg[m % 2], [[1, Tn]])
                S.act(sg, accs[0], AF.Silu)
                S.tt("dve", A.b(o_act + 2 * m * Tn, [[1, Tn]]), sg, accs[1], ALU.mult)
            if up_tok is not None:
                up_tok(js, slot)
        for mp in range(4):
            psd = [S.ps(), S.ps()]
            for kh in range(2):
                sl = slab("down", l, wb_down, kh * 11, 11, 1024, mp * 256, 256)
                for a in range(2):
                    S.mm(psf(psd[a], 0, [[1, Tn]]),
                         [(wv(sl, k, a * 128, 128, 256), A.b(o_act + 2 * (kh * 11 + k) * Tn, [[1, Tn]])) for k in range(11)],
                         start=(kh == 0), stop=(kh == 1))
            for a in range(2):
                m = mp * 2 + a
                xv = A.f(xoff + 4 * m * Tn, [[1, Tn]])
                S.tt("dve", xv, xv, psf(psd[a], 0, [[1, Tn]]), ALU.add)

    def fconv_prompt(ch, ps, r):
        raw = o_uraw[r]
        S.cp("act", A.f(raw + 8, [[1, T]]), psf(ps, 0, [[1, T]]))
        S.cp("act", A.f(raw, [[1, 2]]), A.f(o_fhalo + 8 * ch, [[1, 2]]))
        acc = A.f(o_facc[r], [[1, T]])
        S.ts("dve", acc, A.f(raw + 8, [[1, T]]), par(P_FCW + 3 * ch + 2, [[1, 1]]), par(P_FCB + ch, [[1, 1]]), ALU.mult, ALU.add)
        S.stt("dve", acc, A.f(raw + 4, [[1, T]]), par(P_FCW + 3 * ch + 1, [[1, 1]]), acc, ALU.mult, ALU.add)
        S.stt("dve", acc, A.f(raw, [[1, T]]), par(P_FCW + 3 * ch + 0, [[1, 1]]), acc, ALU.mult, ALU.add)
        S.cp("act", A.f(o_fhalo + 8 * ch, [[1, 2]]), A.f(raw + 4 * T, [[1, 2]]))
        return acc

    def store_prompt(l, i):
        if l < nlayers - 1:
            b = Buf(f"x1_{i}")
            x1buf[i] = b
            S.dma("pool", dram(x1T, i * T, [[SEQ, 128], [128 * SEQ, 8], [1, T]]), A.f(o_xT, [[T, 8], [1, T]]), writes=[b])
        else:
            for tb in range(4):
                yt = o_ytok[tb % 2]
                for half in range(2):
                    ps = S.ps()
                    S.tr([(psf(ps, j * 128, [[1, 128]]), A.f(o_xT + 4 * ((half * 4 + j) * T + tb * 128), [[1, 128]])) for j in range(4)], ident_f)
                    S.cp("act" if half else "dve", A.f(yt + 4 * half * 512, [[1, 512]]), psf(ps, 0, [[1, 512]]))
                S.dma("pool", dram(y_p, (i * T + tb * 128) * D, [[D, 128], [1, D]]), A.f(yt, [[1, 1024]]))

    def layer_end_prompt(l):
        for half in range(2):
            ps = S.ps()
            S.tr([(psf(ps, j * 128, [[1, 128]]), A.f(o_H + 4 * (half * 4 + j) * 128, [[1, 128]])) for j in range(4)], ident_f)
            S.cp("act", A.f(o_ytok[0] + 4 * half * 512, [[1, 512]]), psf(ps, 0, [[1, 512]]))
        S.dma("pool", dram(ssm_p, l * 1024 * 128, [[128, 128], [128 * 128, 8], [1, 128]]), A.f(o_ytok[0], [[128, 8], [1, 128]]))
        for t in range(3):
            S.dma("pool", dram(conv_p, (l * 3 + t) * 1536, [[1, 128], [128, 12]]), A.f(o_halo + 4 * t, [[3, 12]]),
                  allow_slow_non_contiguous=True)
        for t in range(2):
            S.dma("pool", dram(ffn_p, (l * 2 + t) * 2 * DFF, [[1, 128], [128, 44]]), A.f(o_fhalo + 4 * t, [[2, 44]]),
                  allow_slow_non_contiguous=True)

    o_colmask = o_ssb
    o_maskS = o_ssb + 2 * 1024
    o_mnew = o_ssb + 2 * (1024 + 144)
    btri = A.f(o_ssf, [[1, 64]], npart=64)
    bslow = A.f(o_ssf + 4 * 64, [[1, 64]], npart=64)
    ident_f64 = A.f(o_cf, [[1, 64]], npart=64)
    SR = o_kr[0][0]
    o_kTn, o_halos, o_fhalos = SR, SR + 1024, SR + 3584
    o_st = [SR + 9216, SR + 13312]
    o_H0b = [SR + 17408, SR + 19456]
    o_hn, o_CTm, o_Bm, o_dtam, o_cdall, o_sm2 = SR + 21504, SR + 25600, SR + 26112, SR + 26624, SR + 27648, SR + 28672
    o_Vn, o_PnT = SR + 29184, SR + 30720
    assert SR + 32256 <= o_vones
    A.cur = o_U
    o_Kc = A.alloc(9 * 512 * 4)
    o_KT = A.alloc(18 * 128 * 2)
    o_Vcb = A.alloc(9 * 256 * 2)
    o_Es = A.alloc(512)
    o_Ps = A.alloc(512)
    o_ud = A.alloc(1024)

    def mm_multi(items):
        def fn(e, items=items):
            ins = None
            for o, l_, r_ in items:
                ins = e.matmul(o, lhsT=l_, rhs=r_, start=True, stop=True)
            return ins
        S.op("pe", fn, [x for it in items for x in it[1:]], [it[0] for it in items])

    def sample_tile(l):
        Tn = TS
        if l == 0:
            st = o_xtok[0]
            S.dma("sp", A.f(st, [[1, 1024]], npart=64), x_s.ap())
            for half in range(2):
                ps = S.ps()
                S.tr([(psf(ps, j * 64, [[1, 64]]), A.f(st + 4 * (half * 4 + j) * 128, [[1, 128]], npart=64)) for j in range(4)], ident_f64)
                S.cp("act", A.f(o_xTs + 4 * half * 4 * Tn, [[1, 4 * Tn]]), psf(ps, 0, [[1, 256]]))
        stg = o_xtok[1]
        S.memset("dve", A.f(stg, [[1, 512]], npart=64), 0.0)
        for pc in range(3):
            for r in range(3):
                S.dma("sp", A.f(stg, [[1, 512]], p0=16 * r, npart=16),
                      dram(sconv, (l * NB_S * 3 + r) * 1536 + pc * 512, [[3 * 1536, 16], [1, 512]]))
            ps = S.ps()
            S.tr([(psf(ps, j * 64, [[1, 64]]), A.f(stg + 4 * j * 128, [[1, 128]], npart=64)) for j in range(4)], ident_f64)
            S.cp("act", A.f(o_halos + 4 * pc * 4 * 48, [[48, 4], [1, 48]]), psf(ps, 0, [[64, 4], [1, 48]]))
        for pc in range(11):
            for r in range(2):
                S.dma("sp", A.f(stg, [[1, 512]], p0=16 * r, npart=16),
                      dram(sffn, (l * NB_S * 2 + r) * 2 * DFF + pc * 512, [[2 * 2 * DFF, 16], [1, 512]]))
            ps = S.ps()
            S.tr([(psf(ps, j * 32, [[1, 32]]), A.f(stg + 4 * j * 128, [[1, 128]], npart=32)) for j in range(4)], A.f(o_cf, [[1, 32]], npart=32))
            S.cp("act", A.f(o_fhalos + 4 * pc * 4 * 32, [[1, 128]]), psf(ps, 0, [[1, 128]]))

        sstop = cfg.get("sstop", 99)
        if sstop <= 1:
            return
        rmsnorm_fm(o_xTs, Tn, P_GMIX)

        def conv_sample(c, ps):
            raw = o_raw[c % 2]
            S.cp("act", A.f(raw + 4 * 48, [[1, 64]]), psf(ps, 0, [[1, 64]]))
            S.cp("act", A.f(raw, [[1, 48]]), A.f(o_halos + 4 * c * 48, [[1, 48]]))
            acc = A.f(o_acc[c % 2], [[1, 64]])
            S.ts("dve", acc, A.f(raw + 4 * 48, [[1, 64]]), par(P_CW + 4 * c + 3, [[1, 1]]), par(P_CB + c, [[1, 1]]), ALU.mult, ALU.add)
            for tap in (2, 1, 0):
                S.stt("dve", acc, A.f(raw + 4 * 16 * tap, [[1, 64]]), par(P_CW + 4 * c + tap, [[1, 1]]), acc, ALU.mult, ALU.add)
            S.act(A.b(o_xc + 2 * c * Tn, [[1, Tn]]), acc, AF.Silu)

        def xbc_tok(xs, sl):
            ps = S.ps()
            S.mm(psf(ps, 0, [[1, 512]], npart=64), [(A.b(o_hT + 2 * k * Tn, [[1, 64]]), wv(sl, k, 0, 512, 512)) for k in range(8)])
            sg_ = o_xtok[xs % 2]
            S.cp("act", A.f(sg_, [[1, 512]], npart=64), psf(ps, 0, [[1, 512]], npart=64))
            for r in range(3):
                S.dma("pool", dram(conv_s, (l * NB_S * 3 + r) * 1536 + xs * 512, [[3 * 1536, 16], [1, 512]]),
                      A.f(sg_, [[1, 512]], p0=16 * (r + 1), npart=16))
        p1a_proj(l, Tn, 1, 64, conv_sample, xbc_tok)
        if sstop <= 2:
            return

        def ssd_states(psyi):
            if cfg.get("nostates"):
                for g in range(2):
                    S.mm(psf(psyi[g], 0, [[1, 512]], npart=64), [(A.b(o_xc + 2 * 10 * Tn, [[1, 64]]), A.b(o_Hb, [[1, 512]]))])
                return
            nb_lim = cfg.get("nb_lim", NB_S)
            for t_ in psyi:
                S.held.add(int(t_.name[2:]))
            dta = A.f(o_small, [[1, 16]], npart=64)
            dtp = A.f(o_dtp, [[1, 16]], npart=64)
            ps = S.ps()
            S.mm(psf(ps, 0, [[1, 16]], npart=64), [(bslow, dta)])
            dend = A.f(o_sm2, [[1, 16]], npart=64)
            S.act(dend, psf(ps, 0, [[1, 16]], npart=64), AF.Exp)
            wend = A.f(o_sm2 + 64, [[1, 16]], npart=64)
            S.tt("dve", wend, dtp, dend, ALU.mult)
            S.tt("dve", A.b(o_xde, [[64, 16], [1, 64]], npart=64), A.b(o_xtk, [[64, 16], [1, 64]], npart=64),
                 A.f(o_sm2 + 64, [[1, 16], [0, 64]], npart=64), ALU.mult)
            S.tt("dve", A.f(o_dtam, [[16, 16], [1, 16]], npart=64), A.f(o_small, [[0, 16], [1, 16]], npart=64),
                 A.f(o_ssf + 4 * 128, [[1, 16], [0, 16]], npart=64), ALU.mult)
            ps = S.ps()
            S.mm(psf(ps, 0, [[1, 256]]), [(A.f(o_cf + 3 * 512, [[1, 128]], npart=64), A.f(o_dtam, [[1, 256]], npart=64))])
            S.act(A.f(o_cdall, [[1, 256]]), psf(ps, 0, [[1, 256]]), AF.Exp)
            S.memset("dve", A.b(o_Bm, [[1, 256]], p0=64, npart=64), 0.0)
            S.memset("dve", A.b(o_xde, [[1, 1024]], p0=64, npart=64), 0.0)
            for b in range(NB_S):
                st = o_st[b % 2]
                S.dma("sp", A.f(st, [[128, 8], [1, 128]]), dram(sssm, (l * NB_S + b) * 16 * 64 * 128, [[128, 128], [16384, 8], [1, 128]]))
                psT = [S.ps(), S.ps()]
                for half in range(2):
                    S.tr([(psf(psT[half], j * 128, [[1, 128]]), A.f(st + 4 * (half * 4 + j) * 128, [[1, 128]])) for j in range(4)], ident_f)
                Hb = o_H0b[b % 2]
                for half in range(2):
                    S.cp("act", A.b(Hb + 2 * half * 512, [[1, 512]]), psf(psT[half], 0, [[1, 512]]))
                S.tt("dve", A.b(o_CTm, [[64, 2], [1, 64]]), A.b(o_xc + 2 * 10 * Tn, [[Tn, 2], [1, 64]]),
                     A.b(o_colmask + 2 * b * 64, [[0, 2], [1, 64]]), ALU.mult)
                for g in range(2):
                    S.mm(psf(psyi[g], 0, [[1, 512]], npart=64), [(A.b(o_CTm + 2 * g * 64, [[1, 64]]), A.b(Hb + 2 * g * 512, [[1, 512]]))],
                         start=(b == 0), stop=(b == NB_S - 1))
                if cfg.get("st_lvl", 9) <= 1:
                    continue
                S.tt("dve", A.b(o_Bm, [[1, 256]], npart=64), A.b(o_btk, [[1, 256]], npart=64),
                     A.f(o_ssf + 4 * (128 + b), [[0, 256]], npart=64), ALU.mult)
                psS = [S.ps(), S.ps()]
                for g in range(2):
                    S.mm(psf(psS[g], 0, [[1, 512]]), [(A.b(o_Bm + 2 * g * 128, [[1, 128]]), A.b(o_xde + 2 * g * 512, [[1, 512]]))])
                for half in range(2):
                    if cfg.get("dbgA"):
                        continue
                    S.cp("act", A.f(o_hn + 4 * half * 512, [[1, 512]]), psf(psT[half], 0, [[1, 512]]))
                    S.tt("dve", A.f(o_hn + 4 * half * 512, [[64, 8], [1, 64]]), A.f(o_hn + 4 * half * 512, [[64, 8], [1, 64]]),
                         A.f(o_cdall + 4 * (b * 16 + half * 8), [[1, 8], [0, 64]]), ALU.mult)
                    hv = A.f(o_hn + 4 * half * 512, [[1, 512]])
                    if cfg.get("dbgB"):
                        continue
                    S.tt("dve", hv, hv, psf(psS[half], 0, [[1, 512]]), ALU.add)
                if cfg.get("st_lvl", 9) <= 2:
                    continue
                psB = [S.ps(), S.ps()]
                for half in range(2):
                    S.tr([(psf(psB[half], j * 128, [[1, 128]]), A.f(o_hn + 4 * (half * 4 + j) * 128, [[1, 128]])) for j in range(4)], ident_f)
                    S.cp("act", A.f(st + 4 * half * 512, [[1, 512]]), psf(psB[half], 0, [[1, 512]]))
                S.dma("pool", dram(ssm_s, (l * NB_S + b) * 1024 * 128, [[128, 128], [16384, 8], [1, 128]]), A.f(st, [[128, 8], [1, 128]]))
            for t_ in psyi:
                S.held.discard(int(t_.name[2:]))

        ssd_chunk(Tn, 0, 64, btri, bslow, o_ssf, hstate={"yinter": ssd_states, "update": lambda *a_: None})

        if sstop <= 3:
            return
        qk_slabs = [(0, [(False, 0), (False, 1), (False, 2), (False, 3)]),
                    (512, [(False, 4), (False, 5), (True, 0), (True, 1)]),
                    (1024, [(True, 2), (True, 3), (True, 4), (True, 5)])]
        for col0, items in qk_slabs:
            sl = slab("in", l, wb_in, 0, 8, IN_W, col0, 512)
            for j, (is_k, cc) in enumerate(items):
                qk_chunk(sl, j, is_k, cc, 0, Tn, False)

        def out_kv(tb):
            for g in range(3):
                for t in range(4):
                    S.dma("pool", dram(kvs[g], l * TS * 512 + t * 512, [[4 * 512, 16], [256, 2], [1, 256]]),
                          A.f(o_kvst + 4 * g * 256, [[768, 2], [1, 256]], p0=16 * t, npart=16))
            S.cp("act", A.b(o_Vn, [[1, 768]], npart=64), A.f(o_kvst + 4 * 768, [[1, 768]], npart=64))
        kv_natural(l, Tn, 1, 64, out_kv)
        psN = [S.ps(), S.ps()]
        mm_multi([(psf(psN[h % 2], (h // 2) * 64, [[1, 64]], npart=64),
                   A.b(o_kTn + 2 * (h // 2) * Tn, [[1, 64]], p0=64 * (h % 2), npart=64),
                   A.b(o_q + 2 * (h // 2) * Tn, [[1, 64]], p0=64 * (h % 2), npart=64)) for h in range(0, 12, 2)])
        mm_multi([(psf(psN[h % 2], (h // 2) * 64, [[1, 64]], npart=64),
                   A.b(o_kTn + 2 * (h // 2) * Tn, [[1, 64]], p0=64 * (h % 2), npart=64),
                   A.b(o_q + 2 * (h // 2) * Tn, [[1, 64]], p0=64 * (h % 2), npart=64)) for h in range(1, 12, 2)])
        for par_ in range(2):
            S.act(A.b(o_E[0], [[1, 384]], npart=64), psf(psN[par_], 0, [[1, 384]], npart=64), AF.Exp, scale=0.125)
            S.tt("dve", A.b(o_PnT + 2 * par_ * 64, [[128, 6], [1, 64]], npart=64), A.b(o_E[0], [[64, 6], [1, 64]], npart=64),
                 A.b(o_mnew + 2 * par_ * 64, [[128, 6], [1, 64]], npart=64), ALU.mult)
        if sstop <= 4:
            return
        for b in range(NB_S):
            S.dma("sp", A.f(o_Kc, [[1, 512]]), dram(ckv[0], (l * NB_S + b) * 128 * 512, [[512, 128], [1, 512]]))
            S.dma("sp", A.f(o_Kc + 4 * 512, [[512, 4], [1, 512]]), dram(ckv[1], (l * NB_S + b) * 512 * 512, [[4 * 512, 128], [512, 4], [1, 512]]))
            S.dma("sp", A.f(o_Kc + 4 * 5 * 512, [[512, 4], [1, 512]]), dram(ckv[2], (l * NB_S + b) * 2048 * 512, [[16 * 512, 128], [512, 4], [1, 512]]))
            S.cp("pool", A.b(o_Vcb, [[256, 9], [1, 256]]), A.f(o_Kc + 4 * 256, [[512, 9], [1, 256]]))
            allk = [(ch, hp) for ch in range(9) for hp in range(2)]
            for grp in range(5):
                items = allk[grp * 4:(grp + 1) * 4]
                ps = S.ps()
                S.tr([(psf(ps, n * 128, [[1, 128]]), A.f(o_Kc + 4 * (ch * 512 + hp * 128), [[1, 128]])) for n, (ch, hp) in enumerate(items)], ident_f)
                S.cp("act" if grp % 2 else "dve", A.b(o_KT + 2 * grp * 4 * 128, [[1, 128 * len(items)]]), psf(ps, 0, [[1, 128 * len(items)]]))
            psS2 = [S.ps(), S.ps()]
            for p_ in range(2):
                its = []
                for ch in range(9):
                    g = 0 if ch == 0 else (1 if ch < 5 else 2)
                    for hp in range(2):
                        its.append((psf(psS2[p_], (ch * 2 + hp) * 4, [[1, 4]]),
                                    A.b(o_KT + 2 * (ch * 2 + hp) * 128, [[1, 128]], p0=64 * p_, npart=64),
                                    A.b(o_q + 2 * ((2 * g + hp) * Tn + b), [[16, 4]], p0=64 * p_, npart=64)))
                mm_multi(its)
            for p_ in range(2):
                S.act(A.b(o_Es + 2 * p_ * 4, [[16, 9], [8, 2], [1, 4]]), psf(psS2[p_], 0, [[8, 9], [4, 2], [1, 4]]), AF.Exp, scale=0.125)
            S.tt("dve", A.b(o_Ps, [[1, 144]]), A.b(o_Es, [[1, 144]]), A.b(o_maskS, [[1, 144]]), ALU.mult)
            psUD = [S.ps(), S.ps()]
            for h in range(12):
                g, a = h // 4, h % 4
                chs = [0] if g == 0 else (list(range(1, 5)) if g == 1 else list(range(5, 9)))
                rhs_c = [A.b(o_Ps + 2 * (ch * 16 + a * 4), [[1, 4]]) for ch in chs]
                rhs_n = A.b(o_PnT + 2 * (h * 64 + b), [[16, 4]], npart=64)
                S.mm(psf(psUD[0], h * 4, [[1, 4]], npart=64),
                     [(A.b(o_Vcb + 2 * (ch * 256 + a * 64), [[1, 64]]), r_) for ch, r_ in zip(chs, rhs_c)]
                     + [(A.b(o_Vn + 2 * h * 64, [[1, 64]], npart=64), rhs_n)])
                S.mm(psf(psUD[1], h * 4, [[1, 4]], npart=64),
                     [(A.b(o_vones, [[1, 64]]), r_) for r_ in rhs_c] + [(A.b(o_vones, [[1, 64]], npart=64), rhs_n)])
            S.cp("act", A.f(o_ud, [[1, 48]], npart=64), psf(psUD[0], 0, [[1, 48]], npart=64))
            S.cp("dve", A.f(o_ud + 4 * 48, [[1, 48]], npart=64), psf(psUD[1], 0, [[1, 48]], npart=64))
            ds = A.f(o_ud + 4 * 96, [[1, 16]], npart=64)
            S.tt("dve", ds, A.f(o_ud + 4 * 48, [[1, 16]], npart=64), A.f(o_ud + 4 * 64, [[1, 16]], npart=64), ALU.add)
            S.tt("dve", ds, ds, A.f(o_ud + 4 * 80, [[1, 16]], npart=64), ALU.add)
            S.recip(ds, ds)
            S.tt("dve", A.f(o_ud, [[16, 3], [1, 16]], npart=64), A.f(o_ud, [[16, 3], [1, 16]], npart=64),
                 A.f(o_ud + 4 * 96, [[0, 3], [1, 16]], npart=64), ALU.mult)
            for par_ in range(2):
                S.cp("act" if par_ else "dve", A.b(o_mix + 2 * b, [[Tn, 6], [16, 4]], p0=64 * par_, npart=64),
                     A.f(o_ud + 4 * par_ * 4, [[8, 6], [1, 4]], npart=64))

        if sstop <= 5:
            return
        def fconv_sample(ch, ps, r):
            raw = o_uraw[r]
            S.cp("act", A.f(raw + 4 * 32, [[1, 64]]), psf(ps, 0, [[1, 64]]))
            S.cp("act", A.f(raw, [[1, 32]]), A.f(o_fhalos + 4 * ch * 32, [[1, 32]]))
            acc = A.f(o_facc[r], [[1, 64]])
            S.ts("dve", acc, A.f(raw + 4 * 32, [[1, 64]]), par(P_FCW + 3 * ch + 2, [[1, 1]]), par(P_FCB + ch, [[1, 1]]), ALU.mult, ALU.add)
            S.stt("dve", acc, A.f(raw + 4 * 16, [[1, 64]]), par(P_FCW + 3 * ch + 1, [[1, 1]]), acc, ALU.mult, ALU.add)
            S.stt("dve", acc, A.f(raw, [[1, 64]]), par(P_FCW + 3 * ch, [[1, 1]]), acc, ALU.mult, ALU.add)
            return acc

        def up_tok(js, slot):
            ps = S.ps()
            S.mm(psf(ps, 0, [[1, 512]], npart=64), [(A.b(o_hT + 2 * k * Tn, [[1, 64]]), wv(slot, k, 0, 512, 512)) for k in range(8)])
            sg_ = o_ytok[js % 2]
            S.cp("act", A.f(sg_, [[1, 512]], npart=64), psf(ps, 0, [[1, 512]], npart=64))
            for r in range(2):
                for part in range(2):
                    S.dma("pool", dram(ffn_s, (l * NB_S * 2 + r) * 2 * DFF + part * DFF + 256 * js, [[2 * 2 * DFF, 16], [1, 256]]),
                          A.f(sg_ + 4 * part * 256, [[1, 256]], p0=16 * (r + 2), npart=16))
        p2(l, Tn, o_xTs, fconv_sample, up_tok)
        if l == nlayers - 1:
            yt = o_ytok[0]
            for half in range(2):
                ps = S.ps()
                S.tr([(psf(ps, j * 128, [[1, 128]], npart=64), A.f(o_xTs + 4 * (half * 4 + j) * Tn, [[1, Tn]])) for j in range(4)], ident_f)
                S.cp("act", A.f(yt + 4 * half * 512, [[1, 512]], npart=64), psf(ps, 0, [[1, 512]], npart=64))
            for t in range(4):
                S.dma("pool", dram(y_s, t * D, [[4 * D, 16], [1, D]]), A.f(yt, [[1, 1024]], p0=16 * t, npart=16))

    stop = cfg.get("stop", 99)
    for l in range(nlayers):
        load_params(l)
        if stop <= 1:
            break
        S.memset("dve", A.f(o_halo, [[1, 36]]), 0.0)
        S.memset("dve", A.f(o_fhalo, [[1, 88]]), 0.0)
        S.memset("dve", A.f(o_H, [[1, 1024]]), 0.0)
        S.memset("dve", A.b(o_Hb, [[1, 1024]]), 0.0)
        for i in range(ntiles):
            load_x_prompt(l, i)
            rmsnorm_fm(o_xT, T, P_GMIX)
            if stop <= 2:
                continue
            p1a_proj(l, T, 4, 128, conv_prompt)
            if stop <= 3:
                continue
            for tb in range(4):
                ssd_chunk(T, tb, 128, tri_f, slow_f, o_cf + 512)
            if stop <= 4:
                continue
            p1b_prompt(l, i)
            if stop <= 5:
                continue
            p2(l, T, o_xT, fconv_prompt)
            store_prompt(l, i)
        layer_end_prompt(l)
        if do_sample:
            sample_tile(l)
    S.finish()
    S.emit()
    return nc


_CACHE = {}


def run(inputs, cfg):
    key = tuple(sorted(cfg.items()))
    if key not in _CACHE:
        _CACHE[key] = build(cfg)
    nc = _CACHE[key]
    hc = host_consts()
    f = lambda a: np.ascontiguousarray(a, dtype=np.float32)
    in_maps = []
    ncores = cfg.get("ncores", NCORES)
    for c in range(ncores):
        b0 = NB_S * c
        m = {"x_p": f(inputs["x_prompt"][c // 2]),
             "x_s": f(inputs["x_sample"][b0:b0 + NB_S].transpose(1, 0, 2).reshape(TS, D)),
             "ckv0": f(inputs["cache_kv0"][:, b0:b0 + NB_S].reshape(2, NB_S, 128, 512)),
             "ckv1": f(inputs["cache_kv1"][:, b0:b0 + NB_S].reshape(2, NB_S, 512, 512)),
             "ckv2": f(inputs["cache_kv2"][:, b0:b0 + NB_S].reshape(2, NB_S, 2048, 512)),
             "sssm": f(inputs["state_ssm"][:, b0:b0 + NB_S]),
             "sconv": f(inputs["state_conv"][:, b0:b0 + NB_S]),
             "sffn": f(inputs["state_ffn_conv"][:, b0:b0 + NB_S])}
        for n in WNAMES:
            m[n] = f(inputs[n])
        m.update(hc)
        in_maps.append(m)
    res = run_bass_kernel_spmd(nc, in_maps, core_ids=list(range(ncores))).results
    res = list(res) + [res[0]] * (NCORES - ncores)
    ev = [res[2 * b] for b in range(4)]
    y_prompt = np.stack([r["y_p"] for r in ev])
    y_sample = np.concatenate([r["y_s"].reshape(NB_S, 4, D) for r in res], axis=0)
    kvp = [np.stack([r[f"kv{g}_p"].reshape(2, Wg, 2, 4, 64) for r in ev], axis=1) for g, Wg in enumerate((128, 512, 2048))]
    ssm_p = np.stack([r["ssm_p"].reshape(2, 16, 64, 128) for r in ev], axis=1)
    conv_p = np.stack([r["conv_p"] for r in ev], axis=1)
    ffn_p = np.stack([r["ffn_p"] for r in ev], axis=1)
    kvs = [np.concatenate([r[f"kv{g}_s"].reshape(2, NB_S, 4, 2, 4, 64) for r in res], axis=1) for g in range(3)]
    ssm_s = np.concatenate([r["ssm_s"].reshape(2, NB_S, 16, 64, 128) for r in res], axis=1)
    conv_s = np.concatenate([r["conv_s"] for r in res], axis=1)
    ffn_s = np.concatenate([r["ffn_s"] for r in res], axis=1)
    return (y_prompt, y_sample, kvp[0], kvp[1], kvp[2], ssm_p, conv_p, ffn_p,
            kvs[0], kvs[1], kvs[2], ssm_s, conv_s, ffn_s)


def kernel(**inputs):
    return run(inputs, {})
```

```python
import numpy as np
import concourse.bass as bass
import concourse.mybir as mybir
from concourse.bass_utils import run_bass_kernel_spmd

F32 = mybir.dt.float32
BF16 = mybir.dt.bfloat16
AF = mybir.ActivationFunctionType
ALU = mybir.AluOpType
AX = mybir.AxisListType

NCORES = 8
D = 1024
SEQ = 4096
T = 512
NT = SEQ // T
NB_S = 16
TS = 64
IN_W = 4880
DFF = 2816
MIXW = 1792
EPS = 1e-6
PAGE = 512


class Buf:
    __slots__ = ("name", "w", "r")

    def __init__(self, name=""):
        self.name = name
        self.w = None
        self.r = []


class Sched:
    CE = ("pe", "act", "dve", "pool")
    EP = 16000
    NEP = 6

    def __init__(self, nc, n_dma_sp=24, n_dma_pool=40):
        self.nc = nc
        self.lists = {e: [] for e in ("pe", "act", "dve", "pool", "sp")}
        self.cnt = {e: 0 for e in self.CE}
        self.csem = {e: [nc.alloc_semaphore(name=f"c_{e}_{i}") for i in range(self.NEP)] for e in self.CE}
        self.dsem = {"sp": [nc.alloc_semaphore(name=f"d_sp_{i}") for i in range(n_dma_sp)],
                     "pool": [nc.alloc_semaphore(name=f"d_pool_{i}") for i in range(n_dma_pool)]}
        self.dval = {q: [0] * len(self.dsem[q]) for q in self.dsem}
        self.drr = {q: 0 for q in self.dsem}
        self.known = {e: {} for e in self.lists}
        self.pg = {}
        self.psrr = 0
        self.held = set()
        self.psum = [nc.alloc_psum_tensor(f"ps{i}", [128, 512], F32) for i in range(8)]

    def _pages(self, ap):
        name = ap.tensor.name
        if name.startswith("ps"):
            tbl = self.pg.setdefault(name, {})
            b = tbl.get(0)
            if b is None:
                b = tbl[0] = Buf(name)
            return [b]
        esz = mybir.dt.size(ap.dtype)
        dims = ap.ap
        row = dims[0][0]
        lo = ap.offset % row if row > 0 else ap.offset
        hi = lo + 1
        for s, c in dims[1:]:
            hi += (c - 1) * abs(s)
        lo_b = lo * esz
        hi_b = hi * esz
        tbl = self.pg.setdefault(name, {})
        res = []
        for p in range(lo_b // PAGE, (hi_b - 1) // PAGE + 1):
            b = tbl.get(p)
            if b is None:
                b = tbl[p] = Buf(f"{name}:{p}")
            res.append(b)
        return res

    def _bufs(self, items):
        out = []
        for it in items:
            if it is None or isinstance(it, (int, float)):
                continue
            if isinstance(it, Buf):
                out.append(it)
            elif isinstance(it, (list, tuple)):
                out.extend(self._bufs(it))
            else:
                out.extend(self._pages(it))
        return out

    def _need(self, eng, tok, waits):
        if tok is None:
            return
        if tok[0] == "c":
            if eng == "pe" and tok[1] == "pe":
                return
            key = ("c", tok[1])
            v = tok[2]
        else:
            key = ("d", tok[1], tok[2])
            v = tok[3]
        if self.known[eng].get(key, 0) >= v:
            return
        if waits.get(key, 0) < v:
            waits[key] = v

    def _collect(self, eng, reads, writes, extra=()):
        waits = {}
        for b in reads:
            self._need(eng, b.w, waits)
        for b in writes:
            self._need(eng, b.w, waits)
            for t in b.r:
                self._need(eng, t, waits)
        for t in extra:
            self._need(eng, t, waits)
        wl = []
        for key, v in waits.items():
            self.known[eng][key] = v
            if key[0] == "c":
                e = key[1]
                ep = (v - 1) // self.EP
                wl.append((self.csem[e][ep], v - ep * self.EP))
            else:
                wl.append((self.dsem[key[1]][key[2]], v))
        return wl

    def _commit(self, tok, reads, writes):
        for b in reads:
            if len(b.r) > 24:
                d = {}
                for t in b.r:
                    k = t[:2] if t[0] == "c" else t[:3]
                    if k not in d or d[k][-1] < t[-1]:
                        d[k] = t
                b.r = list(d.values())
            b.r.append(tok)
        for b in writes:
            b.w = tok
            b.r = []

    def op(self, eng, fn, reads=(), writes=()):
        reads = self._bufs(reads)
        writes = self._bufs(writes)
        wl = self._collect(eng, reads, writes)
        self.cnt[eng] += 1
        n = self.cnt[eng]
        ep = (n - 1) // self.EP
        assert ep < self.NEP, "too many instructions on " + eng
        self.lists[eng].append((wl, fn, (self.csem[eng][ep], 1)))
        self._commit(("c", eng, n), reads, writes)

    def dma(self, q, out_ap, in_ap, reads=(), writes=(), **kw):
        reads = self._bufs(reads)
        writes = self._bufs(writes)
        if out_ap.tensor.name in ("arena",):
            writes = writes + self._pages(out_ap)
        if in_ap.tensor.name in ("arena",):
            reads = reads + self._pages(in_ap)
        idx = self.drr[q]
        self.drr[q] = (idx + 1) % len(self.dsem[q])
        prev = self.dval[q][idx]
        extra = [("d", q, idx, prev)] if prev > 0 else []
        wl = self._collect(q, reads, writes, extra)
        tgt = prev + 16
        self.dval[q][idx] = tgt

        def fn(eng, out_ap=out_ap, in_ap=in_ap, kw=kw):
            return eng.dma_start(out=out_ap, in_=in_ap, **kw)
        self.lists[q].append((wl, fn, (self.dsem[q][idx], 16)))
        self._commit(("d", q, idx, tgt), reads, writes)

    def finish(self):
        for q in self.dsem:
            wl = []
            for idx, v in enumerate(self.dval[q]):
                if v > 0:
                    wl.append((self.dsem[q][idx], v))
            self.lists[q].append((wl, None, None))

    def emit(self):
        nc = self.nc
        lists = self.lists

        def run(eng, items):
            for wl, fn, inc in items:
                for sem, v in wl:
                    eng.wait_ge(sem, v)
                if fn is not None:
                    ins = fn(eng)
                    ins.then_inc(inc[0], inc[1])

        with nc.Block() as block:
            @block.tensor
            def _(e):
                run(e, lists["pe"])

            @block.scalar
            def _(e):
                run(e, lists["act"])

            @block.vector
            def _(e):
                run(e, lists["dve"])

            @block.gpsimd
            def _(e):
                run(e, lists["pool"])

            @block.sync
            def _(e):
                run(e, lists["sp"])

    def ps(self):
        while True:
            i = self.psrr
            self.psrr = (i + 1) % 8
            if i not in self.held:
                return self.psum[i]

    def ps_hold(self, n):
        out = []
        for _ in range(n):
            t = self.ps()
            self.held.add(int(t.name[2:]))
            out.append(t)
        return out

    def ps_release(self, ts):
        for t in ts:
            self.held.discard(int(t.name[2:]))

    def mm(self, out, pairs, start=True, stop=True):
        reads = [x for p in pairs for x in p]

        def fn(e, out=out, pairs=pairs, start=start, stop=stop):
            n = len(pairs)
            ins = None
            for i, (l, r) in enumerate(pairs):
                ins = e.matmul(out, lhsT=l, rhs=r, start=(start and i == 0), stop=(stop and i == n - 1))
            return ins
        self.op("pe", fn, reads, [out])

    def tr(self, outs_ins, ident):
        reads = [i for _, i in outs_ins] + [ident]
        writes = [o for o, _ in outs_ins]

        def fn(e, oi=outs_ins, ident=ident):
            ins = None
            for o, i in oi:
                ins = e.transpose(o, i, ident)
            return ins
        self.op("pe", fn, reads, writes)

    def act(self, out, in_, func, bias=None, scale=None, accum=None):
        kw = {}
        if bias is not None:
            kw["bias"] = bias
        if scale is not None:
            kw["scale"] = scale
        if accum is not None:
            kw["accum_out"] = accum

        def fn(e, out=out, in_=in_, func=func, kw=kw):
            return e.activation(out=out, in_=in_, func=func, **kw)
        self.op("act", fn, [in_, bias, scale], [out, accum])

    def tt(self, eng, out, a, b, op):
        def fn(e, out=out, a=a, b=b, op=op):
            return e.tensor_tensor(out=out, in0=a, in1=b, op=op)
        self.op(eng, fn, [a, b], [out])

    def stt(self, eng, out, a, scalar, b, op0, op1):
        def fn(e, out=out, a=a, scalar=scalar, b=b, op0=op0, op1=op1):
            return e.scalar_tensor_tensor(out=out, in0=a, scalar=scalar, in1=b, op0=op0, op1=op1)
        self.op(eng, fn, [a, b, scalar], [out])

    def ts(self, eng, out, a, s1, s2, op0, op1=None):
        def fn(e, out=out, a=a, s1=s1, s2=s2, op0=op0, op1=op1):
            if op1 is None:
                return e.tensor_single_scalar(out=out, in_=a, scalar=s1, op=op0)
            return e.tensor_scalar(out=out, in0=a, scalar1=s1, scalar2=s2, op0=op0, op1=op1)
        self.op(eng, fn, [a, s1, s2], [out])

    def cp(self, eng, out, in_):
        if eng == "act":
            return self.act(out, in_, AF.Copy)

        def fn(e, out=out, in_=in_):
            return e.tensor_copy(out=out, in_=in_)
        self.op(eng, fn, [in_], [out])

    def recip(self, out, in_):
        def fn(e, out=out, in_=in_):
            return e.reciprocal(out=out, in_=in_)
        self.op("dve", fn, [in_], [out])

    def memset(self, eng, out, val):
        def fn(e, out=out, val=val):
            return e.memset(out, val)
        self.op(eng, fn, [], [out])

    def red(self, eng, out, in_, op=ALU.add):
        def fn(e, out=out, in_=in_, op=op):
            return e.tensor_reduce(out=out, in_=in_, axis=AX.X, op=op)
        self.op(eng, fn, [in_], [out])


class Arena:
    def __init__(self, nc, nbytes):
        self.nbytes = nbytes
        self.t = nc.alloc_sbuf_tensor("arena", [128, nbytes // 4], F32)
        self.tb = self.t.bitcast(BF16)
        self.rowf = nbytes // 4
        self.rowb = nbytes // 2
        self.cur = 0
        self.hi = 0

    def alloc(self, nbytes):
        off = self.cur
        self.cur = (off + nbytes + PAGE - 1) // PAGE * PAGE
        self.hi = max(self.hi, self.cur)
        assert self.cur <= self.nbytes, (self.cur, self.nbytes)
        return off

    def f(self, off, dims, p0=0, npart=128):
        assert off % 4 == 0
        return bass.AP(self.t, p0 * self.rowf + off // 4, [[self.rowf, npart]] + [list(d) for d in dims])

    def b(self, off, dims, p0=0, npart=128):
        assert off % 2 == 0
        return bass.AP(self.tb, p0 * self.rowb + off // 2, [[self.rowb, npart]] + [list(d) for d in dims])


def psf(ps, col, dims, p0=0, npart=128):
    return bass.AP(ps, p0 * 512 + col, [[512, npart]] + [list(d) for d in dims])


def psb(ps, col, dims, p0=0, npart=128):
    return bass.AP(ps.bitcast(BF16), p0 * 1024 + col, [[1024, npart]] + [list(d) for d in dims])


def dram(t, off, dims):
    return bass.AP(t, off, [list(d) for d in dims])


def _slopes():
    h = np.arange(1, 13, dtype=np.float64)
    return np.exp2(-8.0 * h / 12.0)


def host_consts():
    sl = _slopes()
    p = np.arange(128)[:, None]
    f = np.arange(128)[None, :]
    masks = np.zeros((9, 128, 4, 128), np.float64)
    for g, d in ((0, 1), (1, 4)):
        for a in range(4):
            s = sl[4 * g + a] * d
            m0 = np.where(p <= f, np.exp(-s * (f - p)), 0.0)
            m1 = np.where(f <= p, np.exp(-s * (128 + f - p)), 0.0)
            masks[2 * g + 0, :, a, :] = m0
            masks[2 * g + 1, :, a, :] = m1
    rp, pp = p % 4, p // 4
    rf, pf = f % 4, f // 4
    for rel in range(5):
        dist = 32 * rel + pf - pp
        if rel == 0:
            ok = pp <= pf
        elif rel == 4:
            ok = pf <= pp
        else:
            ok = np.ones_like(dist, bool)
        ok = ok & (rp == rf)
        for a in range(4):
            s = sl[8 + a] * 16
            masks[4 + rel, :, a, :] = np.where(ok, np.exp(-s * dist), 0.0)
    masks = masks.reshape(9, 128, 512).astype(np.float32)
    ident = np.eye(128)
    tri = (p <= f).astype(np.float64)
    slow = (p > f).astype(np.float64)
    ones = np.ones((128, 128))
    cf = np.concatenate([ident, tri, slow, ones], axis=1).astype(np.float32)
    blk = np.zeros((128, 128))
    blk[:64, :64] = 1.0 / 64
    blk[64:, 64:] = 1.0 / 64
    cb = np.concatenate([ident, ones / 1024.0, blk, ones], axis=1).astype(np.float32)
    j = np.arange(64)
    tj, bj = j // 16, j % 16
    same = bj[:, None] == bj[None, :]
    btri = (same & (tj[:, None] <= tj[None, :])).astype(np.float64)
    bslow = (same & (tj[:, None] > tj[None, :])).astype(np.float64)
    rowmask = (bj[:, None] == np.arange(16)[None, :]).astype(np.float64)
    sf = np.zeros((128, 256))
    sf[:64, 0:64] = btri
    sf[:64, 64:128] = bslow
    sf[:64, 128:144] = rowmask
    colmask = np.zeros((128, 16, 64))
    for b in range(16):
        colmask[:, b, :] = (bj == b)[None, :]
    r = np.arange(128)
    maskS = np.zeros((128, 9, 4, 4))
    for a in range(4):
        for t in range(4):
            maskS[:, 0, a, t] = np.where(r >= t, np.exp(-sl[a] * (128 + t - r)), 0.0)
            maskS[:, 1 + t, a, t] = np.exp(-sl[4 + a] * 4 * (128 - r))
            maskS[:, 5 + t, a, t] = np.exp(-sl[8 + a] * 16 * (128 - r))
    mnew = np.zeros((128, 12, 64))
    for h in range(12):
        g = h // 4
        dt_ = tj[None, :] - tj[:, None]
        if g == 0:
            m = np.where(same & (dt_ >= 0), np.exp(-sl[h] * dt_), 0.0)
        else:
            m = (same & (dt_ == 0)).astype(np.float64)
        mnew[:64, h, :] = m
    sb = np.concatenate([colmask.reshape(128, 1024), maskS.reshape(128, 144), mnew.reshape(128, 768)], axis=1)
    return {"cmask": masks, "cst_f": cf, "cst_b": cb, "cst_sf": sf.astype(np.float32), "cst_sb": sb.astype(np.float32)}


WNAMES = ["norm_mix", "w_in", "q_norm", "k_norm", "conv_w", "conv_b", "dt_bias", "a_log", "d_skip",
          "ssm_norm", "w_out", "norm_ffn", "w_up", "ffn_conv_w", "ffn_conv_b", "w_down"]
WSHAPES = {"norm_mix": [2, 1024], "w_in": [2, 1024, IN_W], "q_norm": [2, 64], "k_norm": [2, 64],
           "conv_w": [2, 4, 1536], "conv_b": [2, 1536], "dt_bias": [2, 16], "a_log": [2, 16],
           "d_skip": [2, 16], "ssm_norm": [2, 1024], "w_out": [2, MIXW, 1024], "norm_ffn": [2, 1024],
           "w_up": [2, 1024, 2 * DFF], "ffn_conv_w": [2, 3, 2 * DFF], "ffn_conv_b": [2, 2 * DFF],
           "w_down": [2, DFF, 1024]}


def build(cfg):
    nlayers = cfg.get("nlayers", 2)
    ntiles = cfg.get("ntiles", NT)
    do_sample = cfg.get("sample", True)
    nc = bass.Bass("TRN2", target_bir_lowering=False)
    S = Sched(nc)
    A = Arena(nc, 207 * 1024)

    def din(name, shape):
        return nc.dram_tensor(name, shape, F32, kind="ExternalInput")

    def dout(name, shape):
        return nc.dram_tensor(name, shape, F32, kind="ExternalOutput")

    x_p = din("x_p", [SEQ, D])
    x_s = din("x_s", [TS, D])
    ckv = [din("ckv0", [2, NB_S, 128, 512]), din("ckv1", [2, NB_S, 512, 512]), din("ckv2", [2, NB_S, 2048, 512])]
    sssm = din("sssm", [2, NB_S, 16, 64, 128])
    sconv = din("sconv", [2, NB_S, 3, 1536])
    sffn = din("sffn", [2, NB_S, 2, 2 * DFF])
    W = {n: din(n, WSHAPES[n]) for n in WNAMES}
    cmask_d = din("cmask", [9, 128, 512])
    cstf_d = din("cst_f", [128, 512])
    cstb_d = din("cst_b", [128, 512])
    cstsf_d = din("cst_sf", [128, 256])
    cstsb_d = din("cst_sb", [128, 1936])

    y_p = dout("y_p", [SEQ, D])
    y_s = dout("y_s", [TS, D])
    kvp = [dout("kv0_p", [2, 128, 512]), dout("kv1_p", [2, 512, 512]), dout("kv2_p", [2, 2048, 512])]
    ssm_p = dout("ssm_p", [2, 1024, 128])
    conv_p = dout("conv_p", [2, 3, 1536])
    ffn_p = dout("ffn_p", [2, 2, 2 * DFF])
    kvs = [dout(f"kv{g}_s", [2, TS, 512]) for g in range(3)]
    ssm_s = dout("ssm_s", [2, NB_S, 1024, 128])
    conv_s = dout("conv_s", [2, NB_S, 3, 1536])
    ffn_s = dout("ffn_s", [2, NB_S, 2, 2 * DFF])

    wb_in = nc.dram_tensor("wb_in", [2, 1024, IN_W], BF16)
    wb_out = nc.dram_tensor("wb_out", [2, MIXW, 1024], BF16)
    wb_up = nc.dram_tensor("wb_up", [2, 1024, 2 * DFF], BF16)
    wb_down = nc.dram_tensor("wb_down", [2, DFF, 1024], BF16)
    x1T = nc.dram_tensor("x1T", [8, 128, SEQ], F32)
    wbufs = {}

    def convert(name, src, dst, l, rows, cols, rb):
        bl = []
        for r0 in range(0, rows, rb):
            r1 = min(rows, r0 + rb)
            b = Buf(f"{name}{l}_{r0}")
            S.dma("pool", dram(dst, (l * rows + r0) * cols, [[cols, r1 - r0], [1, cols]]),
                  dram(src, (l * rows + r0) * cols, [[cols, r1 - r0], [1, cols]]), writes=[b])
            bl.append(b)
        wbufs[(name, l)] = bl

    for l in range(nlayers):
        convert("in", W["w_in"], wb_in, l, 1024, IN_W, 128)
        convert("out", W["w_out"], wb_out, l, MIXW, 1024, 448)
        convert("up", W["w_up"], wb_up, l, 1024, 2 * DFF, 128)
        convert("down", W["w_down"], wb_down, l, DFF, 1024, 704)

    o_cf = A.alloc(512 * 4)
    o_cb = A.alloc(512 * 2)
    o_mask = A.alloc(9 * 1024)
    o_par = A.alloc(2048)
    o_kgbc = A.alloc(256)
    o_DI = A.alloc(4096)
    o_xT = A.alloc(8 * T * 4)
    o_hT = A.alloc(8 * T * 2)
    o_mix = A.alloc(14 * T * 2)
    NRING = 4
    o_wr = [A.alloc(8192) for _ in range(NRING)]
    RK = (2, 2, 5)
    o_kr = [[A.alloc(2 * T * 2) for _ in range(RK[g])] for g in range(3)]
    o_vr = [[A.alloc(4 * 256 * 2) for _ in range(RK[g])] for g in range(3)]
    o_vones = A.alloc(128)
    o_halo = A.alloc(12 * 3 * 4)
    o_fhalo = A.alloc(44 * 2 * 4)
    o_H = A.alloc(4096)
    o_Hb = A.alloc(2048)
    o_sq = [A.alloc(T * 2) for _ in range(2)]
    o_rstd = A.alloc(T * 4)
    o_small = A.alloc(1024)
    o_xTs = A.alloc(8 * TS * 4)
    o_ssf = A.alloc(256 * 4)
    o_ssb = A.alloc(1936 * 2)
    base = A.cur

    def cf(i):
        return A.f(o_cf + i * 512, [[1, 128]])
    ident_f, tri_f, slow_f, ones_f = cf(0), cf(1), cf(2), cf(3)

    def cb(i):
        return A.b(o_cb + i * 256, [[1, 128]])
    ident_b, onesdiv_b, blk_b, ones_b = cb(0), cb(1), cb(2), cb(3)

    S.dma("sp", A.f(o_cf, [[1, 512]]), cstf_d.ap())
    S.dma("pool", A.b(o_cb, [[1, 512]]), cstb_d.ap())
    for i in range(9):
        S.dma("pool", A.b(o_mask + i * 1024, [[1, 512]]), dram(cmask_d, i * 128 * 512, [[512, 128], [1, 512]]))
    S.memset("dve", A.b(o_vones, [[1, 64]]), 1.0)
    S.dma("sp", A.f(o_ssf, [[1, 256]]), cstsf_d.ap())
    S.dma("pool", A.b(o_ssb, [[1, 1936]]), cstsb_d.ap())

    P_GMIX, P_GFFN, P_GSSM, P_QG, P_KG, P_CW, P_CB, P_FCW, P_FCB, P_DTB, P_A, P_DSK = (
        0, 8, 16, 24, 25, 32, 80, 96, 228, 272, 288, 304)

    def par(eoff, dims):
        return A.f(o_par + 4 * eoff, dims)

    def load_params(l):
        q = "sp"
        ns = dict(allow_slow_non_contiguous=True)
        S.dma(q, par(P_GMIX, [[1, 8]]), dram(W["norm_mix"], l * 1024, [[1, 128], [128, 8]]), **ns)
        S.dma(q, par(P_GFFN, [[1, 8]]), dram(W["norm_ffn"], l * 1024, [[1, 128], [128, 8]]), **ns)
        S.dma(q, par(P_GSSM, [[1, 8]]), dram(W["ssm_norm"], l * 1024, [[1, 128], [128, 8]]), **ns)
        for hf in range(2):
            S.dma(q, A.f(o_par + 4 * P_QG, [[1, 1]], p0=64 * hf, npart=64), dram(W["q_norm"], l * 64, [[1, 64], [1, 1]]), **ns)
            S.dma(q, A.f(o_par + 4 * P_KG, [[1, 1]], p0=64 * hf, npart=64), dram(W["k_norm"], l * 64, [[1, 64], [1, 1]]), **ns)
        for tp in range(4):
            S.dma(q, par(P_CW + tp, [[4, 12]]), dram(W["conv_w"], (l * 4 + tp) * 1536, [[1, 128], [128, 12]]), **ns)
        S.dma(q, par(P_CB, [[1, 12]]), dram(W["conv_b"], l * 1536, [[1, 128], [128, 12]]), **ns)
        for tp in range(3):
            S.dma(q, par(P_FCW + tp, [[3, 44]]), dram(W["ffn_conv_w"], (l * 3 + tp) * 2 * DFF, [[1, 128], [128, 44]]), **ns)
        S.dma(q, par(P_FCB, [[1, 44]]), dram(W["ffn_conv_b"], l * 2 * DFF, [[1, 128], [128, 44]]), **ns)
        S.dma(q, par(P_DTB, [[1, 16]]), dram(W["dt_bias"], l * 16, [[0, 128], [1, 16]]), **ns)
        S.dma(q, par(P_A, [[1, 16]]), dram(W["a_log"], l * 16, [[0, 128], [1, 16]]), **ns)
        S.dma(q, par(P_DSK, [[1, 16]]), dram(W["d_skip"], l * 16, [[0, 128], [1, 16]]), **ns)
        S.dma(q, A.f(o_kgbc, [[1, 64]]), dram(W["k_norm"], l * 64, [[0, 128], [1, 64]]), **ns)
        S.act(par(P_A, [[1, 16]]), par(P_A, [[1, 16]]), AF.Exp)
        S.ts("dve", par(P_A, [[1, 16]]), par(P_A, [[1, 16]]), -1.0, None, ALU.mult)
        S.tt("dve", A.b(o_DI, [[128, 16], [1, 128]]), A.f(o_cf, [[0, 16], [1, 128]]),
             par(P_DSK, [[1, 16], [0, 128]]), ALU.mult)

    ring = {"i": 0}

    def slab(wkey, l, src, row0, nk, ncols_total, col0, ncols, dst_col0=0, slot=None, width=None):
        if slot is None:
            slot = ring["i"]
            ring["i"] = (slot + 1) % NRING
        width = width or ncols
        rows_total = {"in": 1024, "out": MIXW, "up": 1024, "down": DFF}[wkey]
        src_ap = dram(src, (l * rows_total + row0 * 128) * ncols_total + col0,
                      [[ncols_total, 128], [128 * ncols_total, nk], [1, ncols]])
        dst_ap = A.b(o_wr[slot] + 2 * dst_col0, [[width, nk], [1, ncols]])
        S.dma("sp", dst_ap, src_ap, reads=wbufs[(wkey, l)])
        return slot

    def wv(slot, k, col0, ncols, width):
        return A.b(o_wr[slot] + 2 * (k * width + col0), [[1, ncols]])

    def rmsnorm_fm(xoff, Tn, gpar):
        ps = S.ps()
        for c in range(8):
            sq = A.b(o_sq[c % 2], [[1, Tn]])
            S.act(sq, A.f(xoff + 4 * c * Tn, [[1, Tn]]), AF.Square)
            S.mm(psf(ps, 0, [[1, Tn]]), [(onesdiv_b, sq)], start=(c == 0), stop=(c == 7))
        rstd = A.f(o_rstd, [[1, Tn]])
        S.act(rstd, psf(ps, 0, [[1, Tn]]), AF.Sqrt, bias=EPS)
        S.recip(rstd, rstd)
        for c in range(8):
            S.stt("dve", A.b(o_hT + 2 * c * Tn, [[1, Tn]]), A.f(xoff + 4 * c * Tn, [[1, Tn]]),
                  par(gpar + c, [[1, 1]]), rstd, ALU.mult, ALU.mult)

    A.cur = base
    o_xtok = [A.alloc(4096) for _ in range(2)]
    o_gate = A.alloc(4 * 1024 * 2)
    o_raw = [A.alloc(2560) for _ in range(2)]
    o_acc = [A.alloc(2048) for _ in range(2)]
    o_xc = A.alloc(12 * T * 2)
    o_dtp = A.alloc(4 * 16 * 4)
    o_xtk = A.alloc(2048)
    o_xdt = A.alloc(2048)
    o_xde = A.alloc(2048)
    o_btk = A.alloc(512)
    o_R1 = A.alloc(2048)
    o_LT = [A.alloc(2048) for _ in range(2)]
    o_G = [A.alloc(1024) for _ in range(2)]
    o_CBm = A.alloc(1024)
    o_yis = A.alloc(4096)
    o_yn = A.alloc(2048)
    A.cur = base
    o_q = A.alloc(6 * T * 2)
    o_sqb = [A.alloc(T * 2) for _ in range(2)]
    o_rq = [A.alloc(T * 4) for _ in range(2)]
    o_kvst = A.alloc(1536 * 4)
    o_sqk = A.alloc(768 * 4)
    o_ssk = A.alloc(512)
    o_E = [A.alloc(1024) for _ in range(3)]
    o_P = [A.alloc(1024) for _ in range(5)]
    o_U = A.alloc(6 * T * 4)
    o_Dg = A.alloc(3 * 2 * T * 4)
    A.cur = base
    o_uraw = [A.alloc(2560) for _ in range(4)]
    o_facc = [A.alloc(2048) for _ in range(4)]
    o_sg = [A.alloc(2048) for _ in range(2)]
    o_act = A.alloc(22 * T * 2)
    o_ytok = [A.alloc(4096) for _ in range(2)]
    x1buf = {}

    def sm(k, n=16):
        return A.f(o_small + 4 * k, [[1, n]])

    def load_x_prompt(l, i):
        if l == 0:
            for tb in range(4):
                st = o_xtok[tb % 2]
                S.dma("sp", A.f(st, [[1, 1024]]), dram(x_p, (i * T + tb * 128) * D, [[D, 128], [1, D]]))
                for half in range(2):
                    ps = S.ps()
                    S.tr([(psf(ps, j * 128, [[1, 128]]), A.f(st + 4 * (half * 4 + j) * 128, [[1, 128]])) for j in range(4)], ident_f)
                    S.cp("act" if half == 0 else "dve", A.f(o_xT + 4 * (half * 4 * T + tb * 128), [[T, 4], [1, 128]]),
                         psf(ps, 0, [[128, 4], [1, 128]]))
        else:
            S.dma("pool", A.f(o_xT, [[T, 8], [1, T]]), dram(x1T, i * T, [[SEQ, 128], [128 * SEQ, 8], [1, T]]),
                  reads=[x1buf[i]])

    def p1a_proj(l, Tn, ntb, tbw, conv_fn, xbc_tok=None):
        for zs in range(2):
            sl = slab("in", l, wb_in, 0, 8, IN_W, 2304 + 512 * zs, 512)
            for tb in range(ntb):
                ps = S.ps()
                S.mm(psf(ps, 0, [[1, 512]], npart=tbw),
                     [(A.b(o_hT + 2 * (k * Tn + tb * 128), [[1, tbw]]), wv(sl, k, 0, 512, 512)) for k in range(8)])
                S.act(A.b(o_gate + 2 * (tb * 1024 + zs * 512), [[1, 512]], npart=tbw), psf(ps, 0, [[1, 512]], npart=tbw), AF.Silu)
        for xs in range(3):
            sl = slab("in", l, wb_in, 0, 8, IN_W, 3328 + 512 * xs, 512)
            for j in range(4):
                c = xs * 4 + j
                ps = S.ps()
                S.mm(psf(ps, 0, [[1, Tn]]), [(wv(sl, k, j * 128, 128, 512), A.b(o_hT + 2 * k * Tn, [[1, Tn]])) for k in range(8)])
                conv_fn(c, ps)
            if xbc_tok is not None:
                xbc_tok(xs, sl)
        sl = slab("in", l, wb_in, 0, 8, IN_W, 4864, 16)
        for tb in range(ntb):
            ps = S.ps()
            S.mm(psf(ps, 0, [[1, 16]], npart=tbw),
                 [(A.b(o_hT + 2 * (k * Tn + tb * 128), [[1, tbw]]), wv(sl, k, 0, 16, 16)) for k in range(8)])
            d = A.f(o_dtp + 64 * tb, [[1, 16]], npart=tbw)
            S.tt("dve", d, psf(ps, 0, [[1, 16]], npart=tbw), A.f(o_par + 4 * P_DTB, [[1, 16]], npart=tbw), ALU.add)
            S.act(d, d, AF.Exp)
            S.act(d, d, AF.Ln, bias=1.0)

    def conv_prompt(c, ps):
        raw = o_raw[c % 2]
        S.cp("act", A.f(raw + 12, [[1, T]]), psf(ps, 0, [[1, T]]))
        S.cp("act", A.f(raw, [[1, 3]]), A.f(o_halo + 12 * c, [[1, 3]]))
        acc = A.f(o_acc[c % 2], [[1, T]])
        S.ts("dve", acc, A.f(raw + 12, [[1, T]]), par(P_CW + 4 * c + 3, [[1, 1]]), par(P_CB + c, [[1, 1]]), ALU.mult, ALU.add)
        for tap in (2, 1, 0):
            S.stt("dve", acc, A.f(raw + 4 * tap, [[1, T]]), par(P_CW + 4 * c + tap, [[1, 1]]), acc, ALU.mult, ALU.add)
        S.cp("act", A.f(o_halo + 12 * c, [[1, 3]]), A.f(raw + 4 * T, [[1, 3]]))
        S.act(A.b(o_xc + 2 * c * T, [[1, T]]), acc, AF.Silu)

    def ssd_chunk(Tn, tb, Pn, tri_ap, slow_ap, tri_off, hstate=None):
        cols = tb * 128
        dtp = A.f(o_dtp + 64 * tb, [[1, 16]], npart=Pn)
        dta = A.f(o_small, [[1, 16]], npart=Pn)

        def smp(k, n=16):
            return A.f(o_small + 4 * k, [[1, n]], npart=Pn)
        S.tt("dve", dta, dtp, A.f(o_par + 4 * P_A, [[1, 16]], npart=Pn), ALU.mult)
        ps = S.ps()
        S.mm(psf(ps, 0, [[1, 16]], npart=Pn), [(tri_ap, dta)])
        S.act(smp(32), psf(ps, 0, [[1, 16]], npart=Pn), AF.Exp)
        ps = S.ps()
        S.tr([(psb(ps, c * 128, [[1, 128]], npart=Pn), A.b(o_xc + 2 * (c * Tn + cols), [[1, Pn]])) for c in range(8)], ident_b)
        S.cp("act", A.b(o_xtk, [[1, 1024]], npart=Pn), psb(ps, 0, [[1, 1024]], npart=Pn))
        ps = S.ps()
        S.tr([(psb(ps, g * 128, [[1, 128]], npart=Pn), A.b(o_xc + 2 * ((8 + g) * Tn + cols), [[1, Pn]])) for g in range(2)], ident_b)
        S.cp("act", A.b(o_btk, [[1, 256]], npart=Pn), psb(ps, 0, [[1, 256]], npart=Pn))
        S.tt("pool", A.b(o_xdt, [[64, 16], [1, 64]], npart=Pn), A.b(o_xtk, [[64, 16], [1, 64]], npart=Pn),
             A.f(o_dtp + 64 * tb, [[1, 16], [0, 64]], npart=Pn), ALU.mult)
        ps = S.ps()
        for g in range(2):
            S.mm(psf(ps, g * 128, [[1, Pn]], npart=Pn),
                 [(A.b(o_xc + 2 * ((8 + g) * Tn + cols), [[1, Pn]]), A.b(o_xc + 2 * ((10 + g) * Tn + cols), [[1, Pn]]))])
        S.tt("dve", A.f(o_CBm, [[128, 2], [1, Pn]], npart=Pn), psf(ps, 0, [[128, 2], [1, Pn]], npart=Pn),
             A.f(tri_off, [[0, 2], [1, Pn]], npart=Pn), ALU.mult)
        psyi = [S.ps(), S.ps()]
        if hstate is None:
            for g in range(2):
                S.mm(psf(psyi[g], 0, [[1, 512]], npart=Pn),
                     [(A.b(o_xc + 2 * ((10 + g) * Tn + cols), [[1, Pn]]), A.b(o_Hb + 2 * g * 512, [[1, 512]]))])
        else:
            hstate["yinter"](psyi)
        for g in range(2):
            S.tt("dve", A.f(o_yis + 4 * g * 512, [[64, 8], [1, 64]], npart=Pn), psf(psyi[g], 0, [[64, 8], [1, 64]], npart=Pn),
                 A.f(o_small + 4 * (32 + 8 * g), [[1, 8], [0, 64]], npart=Pn), ALU.mult)
        psy = [S.ps(), S.ps()]
        for hg in range(4):
            g = hg // 2
            S.tt("pool", A.f(o_R1, [[128, 4], [1, Pn]], npart=Pn), A.f(tri_off, [[0, 4], [1, Pn]], npart=Pn),
                 A.f(o_small + 4 * (4 * hg), [[1, 4], [0, Pn]], npart=Pn), ALU.mult)
            ps = S.ps()
            S.mm(psf(ps, 0, [[1, 512]], npart=Pn), [(slow_ap, A.f(o_R1, [[1, 512]], npart=Pn))])
            LT = o_LT[hg % 2]
            S.act(A.f(LT, [[1, 512]], npart=Pn), psf(ps, 0, [[1, 512]], npart=Pn), AF.Exp)
            if hstate is None:
                S.cp("act", A.f(o_small + 4 * (64 + 4 * hg), [[1, 4]], npart=Pn), A.f(LT + 4 * (Pn - 1), [[128, 4]], npart=Pn))
            G = o_G[hg % 2]
            S.tt("dve", A.b(G, [[128, 4], [1, Pn]], npart=Pn), A.f(LT, [[128, 4], [1, Pn]], npart=Pn),
                 A.f(o_CBm + 4 * g * 128, [[0, 4], [1, Pn]], npart=Pn), ALU.mult)
            for a in range(4):
                h = 4 * hg + a
                S.mm(psf(psy[h // 8], (h % 8) * 64, [[1, 64]], npart=Pn),
                     [(A.b(G + 2 * a * 128, [[1, Pn]], npart=Pn), A.b(o_xdt + 2 * h * 64, [[1, 64]], npart=Pn)),
                      (A.b(o_DI + 2 * h * 128, [[1, Pn]], npart=Pn), A.b(o_xtk + 2 * h * 64, [[1, 64]], npart=Pn))])
        for g in range(2):
            yv = A.f(o_yis + 4 * g * 512, [[1, 512]], npart=Pn)
            S.tt("dve", yv, psf(psy[g], 0, [[1, 512]], npart=Pn), yv, ALU.add)
        yall = A.f(o_yis, [[1, 1024]], npart=Pn)
        S.tt("dve", yall, yall, A.b(o_gate + 2 * tb * 1024, [[1, 1024]], npart=Pn), ALU.mult)
        S.act(A.b(o_yn, [[1, 1024]], npart=Pn), yall, AF.Square, accum=smp(96, 1))
        S.act(smp(97, 1), smp(96, 1), AF.Sqrt, bias=EPS, scale=1.0 / 1024)
        S.recip(smp(97, 1), smp(97, 1))
        S.ts("dve", A.b(o_yn, [[1, 1024]], npart=Pn), yall, smp(97, 1), None, ALU.mult)
        ps = S.ps()
        S.tr([(psb(ps, c * 128, [[1, Pn]]), A.b(o_yn + 2 * c * 128, [[1, 128]], npart=Pn)) for c in range(8)], A.b(o_cb, [[1, Pn]], npart=Pn))
        S.tt("dve", A.b(o_mix + 2 * (6 * Tn + cols), [[Tn, 8], [1, Pn]]), psb(ps, 0, [[128, 8], [1, Pn]]),
             par(P_GSSM, [[1, 8], [0, Pn]]), ALU.mult)
        if hstate is None:
            ps = S.ps()
            S.mm(psf(ps, 16, [[1, 16]]), [(ones_f, dta)])
            S.act(smp(48), psf(ps, 16, [[1, 16]]), AF.Exp)
            S.tt("pool", smp(80), dtp, smp(64), ALU.mult)
            S.tt("pool", A.b(o_xde, [[64, 16], [1, 64]]), A.b(o_xtk, [[64, 16], [1, 64]]),
                 A.f(o_small + 4 * 80, [[1, 16], [0, 64]]), ALU.mult)
            psS = [S.ps(), S.ps()]
            for g in range(2):
                S.mm(psf(psS[g], 0, [[1, 512]]), [(A.b(o_btk + 2 * g * 128, [[1, 128]]), A.b(o_xde + 2 * g * 512, [[1, 512]]))])
            Hv = A.f(o_H, [[64, 16], [1, 64]])
            S.tt("dve", Hv, Hv, A.f(o_small + 4 * 48, [[1, 16], [0, 64]]), ALU.mult)
            for g in range(2):
                hv = A.f(o_H + 4 * g * 512, [[1, 512]])
                S.tt("dve", hv, hv, psf(psS[g], 0, [[1, 512]]), ALU.add)
            S.cp("act", A.b(o_Hb, [[1, 1024]]), A.f(o_H, [[1, 1024]]))
        else:
            hstate["update"](dta, dtp)
    def mi(g, rel):
        return (rel, 2 + rel, 4 + rel)[g]

    def qk_chunk(sl, j, is_k, cc, i, Tn, permute):
        ps = S.ps()
        S.mm(psf(ps, 0, [[1, Tn]]), [(wv(sl, k, j * 128, 128, 512), A.b(o_hT + 2 * k * Tn, [[1, Tn]])) for k in range(8)])
        sqb = A.b(o_sqb[cc % 2], [[1, Tn]])
        S.act(sqb, psf(ps, 0, [[1, Tn]]), AF.Square)
        ps2 = S.ps()
        S.mm(psf(ps2, 0, [[1, Tn]]), [(blk_b, sqb)])
        rq = A.f(o_rq[cc % 2], [[1, Tn]])
        S.act(rq, psf(ps2, 0, [[1, Tn]]), AF.Sqrt, bias=EPS)
        S.recip(rq, rq)
        g = cc // 2
        if is_k:
            dst = o_kr[g][i % RK[g]] + 2 * (cc % 2) * Tn if permute else o_kr[0][0] + 2 * cc * Tn
        else:
            dst = o_q + 2 * cc * Tn
        gp = par(P_KG if is_k else P_QG, [[1, 1]])
        if g == 0 or not permute:
            S.stt("dve", A.b(dst, [[1, Tn]]), psf(ps, 0, [[1, Tn]]), gp, rq, ALU.mult, ALU.mult)
        else:
            S.stt("dve", A.b(dst, [[1, 128], [128, 4]]), psf(ps, 0, [[4, 128], [1, 4]]), gp,
                  A.f(o_rq[cc % 2], [[4, 128], [1, 4]]), ALU.mult, ALU.mult)

    def kv_natural(l, Tn, ntb, tbw, out_fn):
        sls = [slab("in", l, wb_in, 0, 8, IN_W, 768 + 512 * n, 512) for n in range(3)]
        for tb in range(ntb):
            for n in range(3):
                ps = S.ps()
                S.mm(psf(ps, 0, [[1, 512]], npart=tbw),
                     [(A.b(o_hT + 2 * (k * Tn + tb * 128), [[1, tbw]]), wv(sls[n], k, 0, 512, 512)) for k in range(8)])
                S.cp("act", A.f(o_kvst + 4 * n * 512, [[1, 512]], npart=tbw), psf(ps, 0, [[1, 512]], npart=tbw))
            kv = A.f(o_kvst, [[1, 768]], npart=tbw)
            kv3 = A.f(o_kvst, [[64, 12], [1, 64]], npart=tbw)
            S.tt("pool", A.f(o_sqk, [[1, 768]], npart=tbw), kv, kv, ALU.mult)
            ssk = A.f(o_ssk, [[1, 12]], npart=tbw)
            S.red("dve", ssk, A.f(o_sqk, [[64, 12], [1, 64]], npart=tbw))
            S.act(ssk, ssk, AF.Sqrt, bias=EPS, scale=1.0 / 64)
            S.recip(ssk, ssk)
            S.tt("dve", kv3, kv3, A.f(o_ssk, [[1, 12], [0, 64]], npart=tbw), ALU.mult)
            S.tt("dve", kv3, kv3, A.f(o_kgbc, [[0, 12], [1, 64]], npart=tbw), ALU.mult)
            out_fn(tb)

    def p1b_prompt(l, i):
        qk_slabs = [(0, [(False, 0), (False, 1), (False, 2), (False, 3)]),
                    (512, [(False, 4), (False, 5), (True, 0), (True, 1)]),
                    (1024, [(True, 2), (True, 3), (True, 4), (True, 5)])]
        for col0, items in qk_slabs:
            sl = slab("in", l, wb_in, 0, 8, IN_W, col0, 512)
            for j, (is_k, cc) in enumerate(items):
                qk_chunk(sl, j, is_k, cc, i, T, True)
        lvl = cfg.get("p1b", 99)
        if lvl <= 1:
            return
        slv0 = slab("in", l, wb_in, 0, 8, IN_W, 1536, 512)
        slv1 = slab("in", l, wb_in, 0, 8, IN_W, 2048, 256)
        for g in range(3):
            sl, c0, width = ((slv0, 0, 512), (slv0, 256, 512), (slv1, 0, 256))[g]
            for u in range(4):
                ps = S.ps()
                if g == 0:
                    lhs = [A.b(o_hT + 2 * (k * T + u * 128), [[1, 128]]) for k in range(8)]
                else:
                    lhs = [A.b(o_hT + 2 * (k * T + u), [[4, 128]]) for k in range(8)]
                S.mm(psf(ps, 0, [[1, 256]]), [(lhs[k], wv(sl, k, c0, 256, width)) for k in range(8)])
                S.cp("act", A.b(o_vr[g][i % RK[g]] + 2 * u * 256, [[1, 256]]), psf(ps, 0, [[1, 256]]))
        if i >= NT - 4:
            def out_fn(tb):
                tok = i * T + tb * 128
                for g, Wg in enumerate((128, 512, 2048)):
                    r0 = tok - (SEQ - Wg)
                    if r0 >= 0:
                        S.dma("pool", dram(kvp[g], (l * Wg + r0) * 512, [[512, 128], [256, 2], [1, 256]]),
                              A.f(o_kvst + 4 * g * 256, [[768, 2], [1, 256]]))
            kv_natural(l, T, 4, 128, out_fn)
        if lvl <= 2:
            return
        for g in range(3):
            if lvl <= 3 + g:
                return
            cur = i % RK[g]
            for u in range(4):
                if g == 0:
                    kcs = []
                    if u > 0:
                        kcs.append((cur, u - 1, 1))
                    elif i > 0:
                        kcs.append(((i - 1) % RK[0], 3, 1))
                    kcs.append((cur, u, 0))
                elif g == 1:
                    kcs = ([((i - 1) % RK[1], u, 1)] if i > 0 else []) + [(cur, u, 0)]
                else:
                    kcs = [((i - di) % RK[2], u, di) for di in range(4, -1, -1) if i - di >= 0]
                psU, psD = S.ps_hold(2)
                nk = len(kcs)
                for n, (slot, ku, rel) in enumerate(kcs):
                    psS2 = [S.ps(), S.ps()]
                    for a in range(4):
                        cc2, pb = a // 2, 64 * (a % 2)
                        S.mm(psf(psS2[a % 2], cc2 * 128, [[1, 128]]),
                             [(A.b(o_kr[g][slot] + 2 * (cc2 * T + ku * 128), [[1, 128]], p0=pb, npart=64),
                               A.b(o_q + 2 * ((2 * g + cc2) * T + u * 128), [[1, 128]], p0=pb, npart=64))])
                    E = A.b(o_E[n % 3], [[1, 512]])
                    for hh in (1, 0):
                        S.act(A.b(o_E[n % 3] + 2 * hh * 128, [[256, 2], [1, 128]]), psf(psS2[hh], 0, [[128, 2], [1, 128]]),
                              AF.Exp, scale=0.125)
                    S.tt("pool" if n % 2 == 0 else "dve", A.b(o_P[n], [[1, 512]]), E,
                         A.b(o_mask + 1024 * mi(g, rel), [[1, 512]]), ALU.mult)
                for a in range(4):
                    S.mm(psf(psU, a * 128, [[1, 128]], npart=64),
                         [(A.b(o_vr[g][slot] + 2 * (ku * 256 + a * 64), [[1, 64]]), A.b(o_P[n] + 2 * a * 128, [[1, 128]]))
                          for n, (slot, ku, rel) in enumerate(kcs)])
                S.mm(psf(psD, 0, [[1, 512]], npart=64), [(A.b(o_vones, [[1, 64]]), A.b(o_P[n], [[1, 512]])) for n in range(nk)])
                for par_ in range(2):
                    srcD = psf(psD, par_ * 128, [[256, 2], [1, 128]], npart=64)
                    srcU = psf(psU, par_ * 128, [[256, 2], [1, 128]], npart=64)
                    if g == 0:
                        dstD = A.f(o_Dg + 4 * (g * 2 * T + u * 128), [[T, 2], [1, 128]], p0=64 * par_, npart=64)
                        dstU = A.f(o_U + 4 * (g * 2 * T + u * 128), [[T, 2], [1, 128]], p0=64 * par_, npart=64)
                    else:
                        dstD = A.f(o_Dg + 4 * (g * 2 * T + u), [[T, 2], [4, 128]], p0=64 * par_, npart=64)
                        dstU = A.f(o_U + 4 * (g * 2 * T + u), [[T, 2], [4, 128]], p0=64 * par_, npart=64)
                    S.cp("act" if par_ else "dve", dstD, srcD)
                    S.cp("dve" if par_ else "act", dstU, srcU)
                S.ps_release([psU, psD])
        att_finish(T)

    def att_finish(Tn):
        def Dall(g):
            return A.f(o_Dg + 4 * g * 2 * Tn, [[1, 2 * Tn]])
        S.tt("dve", Dall(0), Dall(0), Dall(1), ALU.add)
        S.tt("dve", Dall(0), Dall(0), Dall(2), ALU.add)
        S.recip(Dall(0), Dall(0))
        for cc in range(6):
            S.tt("pool" if cc % 2 else "dve", A.b(o_mix + 2 * cc * Tn, [[1, Tn]]), A.f(o_U + 4 * cc * Tn, [[1, Tn]]),
                 A.f(o_Dg + 4 * (cc % 2) * Tn, [[1, Tn]]), ALU.mult)

    def p2(l, Tn, xoff, conv_fn, up_tok=None):
        for ms in range(4):
            sl = slab("out", l, wb_out, 0, 14, 1024, ms * 256, 256)
            for j in range(2):
                m = ms * 2 + j
                ps = S.ps()
                S.mm(psf(ps, 0, [[1, Tn]]), [(wv(sl, k, j * 128, 128, 256), A.b(o_mix + 2 * k * Tn, [[1, Tn]])) for k in range(14)])
                xv = A.f(xoff + 4 * m * Tn, [[1, Tn]])
                S.tt("dve", xv, xv, psf(ps, 0, [[1, Tn]]), ALU.add)
        rmsnorm_fm(xoff, Tn, P_GFFN)
        for js in range(11):
            slot = slab("up", l, wb_up, 0, 8, 2 * DFF, 256 * js, 256, dst_col0=0, width=512)
            slab("up", l, wb_up, 0, 8, 2 * DFF, DFF + 256 * js, 256, dst_col0=256, slot=slot, width=512)
            for a in range(2):
                m = 2 * js + a
                accs = []
                for which, cbase, ch in ((0, a * 128, m), (1, 256 + a * 128, 22 + m)):
                    ps = S.ps()
                    S.mm(psf(ps, 0, [[1, Tn]]), [(wv(slot, k, cbase, 128, 512), A.b(o_hT + 2 * k * Tn, [[1, Tn]])) for k in range(8)])
                    accs.append(conv_fn(ch, ps, (2 * m + which) % 4))
                sg = A.f(o_sg[m % 2], [[1, Tn]])
                S.act(sg, accs[0], AF.Silu)
                S.tt("dve", A.b(o_act + 2 * m * Tn, [[1, Tn]]), sg, accs[1], ALU.mult)
            if up_tok is not None:
                up_tok(js, slot)
        for mp in range(4):
            psd = [S.ps(), S.ps()]
            for kh in range(2):
                sl = slab("down", l, wb_down, kh * 11, 11, 1024, mp * 256, 256)
                for a in range(2):
                    S.mm(psf(psd[a], 0, [[1, Tn]]),
                         [(wv(sl, k, a * 128, 128, 256), A.b(o_act + 2 * (kh * 11 + k) * Tn, [[1, Tn]])) for k in range(11)],
                         start=(kh == 0), stop=(kh == 1))
            for a in range(2):
                m = mp * 2 + a
                xv = A.f(xoff + 4 * m * Tn, [[1, Tn]])
                S.tt("dve", xv, xv, psf(psd[a], 0, [[1, Tn]]), ALU.add)

    def fconv_prompt(ch, ps, r):
        raw = o_uraw[r]
        S.cp("act", A.f(raw + 8, [[1, T]]), psf(ps, 0, [[1, T]]))
        S.cp("act", A.f(raw, [[1, 2]]), A.f(o_fhalo + 8 * ch, [[1, 2]]))
        acc = A.f(o_facc[r], [[1, T]])
        S.ts("dve", acc, A.f(raw + 8, [[1, T]]), par(P_FCW + 3 * ch + 2, [[1, 1]]), par(P_FCB + ch, [[1, 1]]), ALU.mult, ALU.add)
        S.stt("dve", acc, A.f(raw + 4, [[1, T]]), par(P_FCW + 3 * ch + 1, [[1, 1]]), acc, ALU.mult, ALU.add)
        S.stt("dve", acc, A.f(raw, [[1, T]]), par(P_FCW + 3 * ch + 0, [[1, 1]]), acc, ALU.mult, ALU.add)
        S.cp("act", A.f(o_fhalo + 8 * ch, [[1, 2]]), A.f(raw + 4 * T, [[1, 2]]))
        return acc

    def store_prompt(l, i):
        if l < nlayers - 1:
            b = Buf(f"x1_{i}")
            x1buf[i] = b
            S.dma("pool", dram(x1T, i * T, [[SEQ, 128], [128 * SEQ, 8], [1, T]]), A.f(o_xT, [[T, 8], [1, T]]), writes=[b])
        else:
            for tb in range(4):
                yt = o_ytok[tb % 2]
                for half in range(2):
                    ps = S.ps()
                    S.tr([(psf(ps, j * 128, [[1, 128]]), A.f(o_xT + 4 * ((half * 4 + j) * T + tb * 128), [[1, 128]])) for j in range(4)], ident_f)
                    S.cp("act" if half else "dve", A.f(yt + 4 * half * 512, [[1, 512]]), psf(ps, 0, [[1, 512]]))
                S.dma("pool", dram(y_p, (i * T + tb * 128) * D, [[D, 128], [1, D]]), A.f(yt, [[1, 1024]]))

    def layer_end_prompt(l):
        for half in range(2):
            ps = S.ps()
            S.tr([(psf(ps, j * 128, [[1, 128]]), A.f(o_H + 4 * (half * 4 + j) * 128, [[1, 128]])) for j in range(4)], ident_f)
            S.cp("act", A.f(o_ytok[0] + 4 * half * 512, [[1, 512]]), psf(ps, 0, [[1, 512]]))
        S.dma("pool", dram(ssm_p, l * 1024 * 128, [[128, 128], [128 * 128, 8], [1, 128]]), A.f(o_ytok[0], [[128, 8], [1, 128]]))
        for t in range(3):
            S.dma("pool", dram(conv_p, (l * 3 + t) * 1536, [[1, 128], [128, 12]]), A.f(o_halo + 4 * t, [[3, 12]]),
                  allow_slow_non_contiguous=True)
        for t in range(2):
            S.dma("pool", dram(ffn_p, (l * 2 + t) * 2 * DFF, [[1, 128], [128, 44]]), A.f(o_fhalo + 4 * t, [[2, 44]]),
                  allow_slow_non_contiguous=True)

    o_colmask = o_ssb
    o_maskS = o_ssb + 2 * 1024
    o_mnew = o_ssb + 2 * (1024 + 144)
    btri = A.f(o_ssf, [[1, 64]], npart=64)
    bslow = A.f(o_ssf + 4 * 64, [[1, 64]], npart=64)
    ident_f64 = A.f(o_cf, [[1, 64]], npart=64)
    SR = o_kr[0][0]
    o_kTn, o_halos, o_fhalos = SR, SR + 1024, SR + 3584
    o_st = [SR + 9216, SR + 13312]
    o_H0b = [SR + 17408, SR + 19456]
    o_hn, o_CTm, o_Bm, o_dtam, o_cdall, o_sm2 = SR + 21504, SR + 25600, SR + 26112, SR + 26624, SR + 27648, SR + 28672
    o_Vn, o_PnT = SR + 29184, SR + 30720
    assert SR + 32256 <= o_vones
    A.cur = o_U
    o_Kc = A.alloc(9 * 512 * 4)
    o_KT = A.alloc(18 * 128 * 2)
    o_Vcb = A.alloc(9 * 256 * 2)
    o_Es = A.alloc(512)
    o_Ps = A.alloc(512)
    o_ud = A.alloc(1024)

    def mm_multi(items):
        def fn(e, items=items):
            ins = None
            for o, l_, r_ in items:
                ins = e.matmul(o, lhsT=l_, rhs=r_, start=True, stop=True)
            return ins
        S.op("pe", fn, [x for it in items for x in it[1:]], [it[0] for it in items])

    def sample_tile(l):
        Tn = TS
        if l == 0:
            st = o_xtok[0]
            S.dma("sp", A.f(st, [[1, 1024]], npart=64), x_s.ap())
            for half in range(2):
                ps = S.ps()
                S.tr([(psf(ps, j * 64, [[1, 64]]), A.f(st + 4 * (half * 4 + j) * 128, [[1, 128]], npart=64)) for j in range(4)], ident_f64)
                S.cp("act", A.f(o_xTs + 4 * half * 4 * Tn, [[1, 4 * Tn]]), psf(ps, 0, [[1, 256]]))
        stg = o_xtok[1]
        S.memset("dve", A.f(stg, [[1, 512]], npart=64), 0.0)
        for pc in range(3):
            for r in range(3):
                S.dma("sp", A.f(stg, [[1, 512]], p0=16 * r, npart=16),
                      dram(sconv, (l * NB_S * 3 + r) * 1536 + pc * 512, [[3 * 1536, 16], [1, 512]]))
            ps = S.ps()
            S.tr([(psf(ps, j * 64, [[1, 64]]), A.f(stg + 4 * j * 128, [[1, 128]], npart=64)) for j in range(4)], ident_f64)
            S.cp("act", A.f(o_halos + 4 * pc * 4 * 48, [[48, 4], [1, 48]]), psf(ps, 0, [[64, 4], [1, 48]]))
        for pc in range(11):
            for r in range(2):
                S.dma("sp", A.f(stg, [[1, 512]], p0=16 * r, npart=16),
                      dram(sffn, (l * NB_S * 2 + r) * 2 * DFF + pc * 512, [[2 * 2 * DFF, 16], [1, 512]]))
            ps = S.ps()
            S.tr([(psf(ps, j * 32, [[1, 32]]), A.f(stg + 4 * j * 128, [[1, 128]], npart=32)) for j in range(4)], A.f(o_cf, [[1, 32]], npart=32))
            S.cp("act", A.f(o_fhalos + 4 * pc * 4 * 32, [[1, 128]]), psf(ps, 0, [[1, 128]]))

        sstop = cfg.get("sstop", 99)
        if sstop <= 1:
            return
        rmsnorm_fm(o_xTs, Tn, P_GMIX)

        def conv_sample(c, ps):
            raw = o_raw[c % 2]
            S.cp("act", A.f(raw + 4 * 48, [[1, 64]]), psf(ps, 0, [[1, 64]]))
            S.cp("act", A.f(raw, [[1, 48]]), A.f(o_halos + 4 * c * 48, [[1, 48]]))
            acc = A.f(o_acc[c % 2], [[1, 64]])
            S.ts("dve", acc, A.f(raw + 4 * 48, [[1, 64]]), par(P_CW + 4 * c + 3, [[1, 1]]), par(P_CB + c, [[1, 1]]), ALU.mult, ALU.add)
            for tap in (2, 1, 0):
                S.stt("dve", acc, A.f(raw + 4 * 16 * tap, [[1, 64]]), par(P_CW + 4 * c + tap, [[1, 1]]), acc, ALU.mult, ALU.add)
            S.act(A.b(o_xc + 2 * c * Tn, [[1, Tn]]), acc, AF.Silu)

        def xbc_tok(xs, sl):
            ps = S.ps()
            S.mm(psf(ps, 0, [[1, 512]], npart=64), [(A.b(o_hT + 2 * k * Tn, [[1, 64]]), wv(sl, k, 0, 512, 512)) for k in range(8)])
            sg_ = o_xtok[xs % 2]
            S.cp("act", A.f(sg_, [[1, 512]], npart=64), psf(ps, 0, [[1, 512]], npart=64))
            for r in range(3):
                S.dma("pool", dram(conv_s, (l * NB_S * 3 + r) * 1536 + xs * 512, [[3 * 1536, 16], [1, 512]]),
                      A.f(sg_, [[1, 512]], p0=16 * (r + 1), npart=16))
        p1a_proj(l, Tn, 1, 64, conv_sample, xbc_tok)
        if sstop <= 2:
            return

        def ssd_states(psyi):
            if cfg.get("nostates"):
                for g in range(2):
                    S.mm(psf(psyi[g], 0, [[1, 512]], npart=64), [(A.b(o_xc + 2 * 10 * Tn, [[1, 64]]), A.b(o_Hb, [[1, 512]]))])
                return
            nb_lim = cfg.get("nb_lim", NB_S)
            for t_ in psyi:
                S.held.add(int(t_.name[2:]))
            dta = A.f(o_small, [[1, 16]], npart=64)
            dtp = A.f(o_dtp, [[1, 16]], npart=64)
            ps = S.ps()
            S.mm(psf(ps, 0, [[1, 16]], npart=64), [(bslow, dta)])
            dend = A.f(o_sm2, [[1, 16]], npart=64)
            S.act(dend, psf(ps, 0, [[1, 16]], npart=64), AF.Exp)
            wend = A.f(o_sm2 + 64, [[1, 16]], npart=64)
            S.tt("dve", wend, dtp, dend, ALU.mult)
            S.tt("dve", A.b(o_xde, [[64, 16], [1, 64]], npart=64), A.b(o_xtk, [[64, 16], [1, 64]], npart=64),
                 A.f(o_sm2 + 64, [[1, 16], [0, 64]], npart=64), ALU.mult)
            S.tt("dve", A.f(o_dtam, [[16, 16], [1, 16]], npart=64), A.f(o_small, [[0, 16], [1, 16]], npart=64),
                 A.f(o_ssf + 4 * 128, [[1, 16], [0, 16]], npart=64), ALU.mult)
            ps = S.ps()
            S.mm(psf(ps, 0, [[1, 256]]), [(A.f(o_cf + 3 * 512, [[1, 128]], npart=64), A.f(o_dtam, [[1, 256]], npart=64))])
            S.act(A.f(o_cdall, [[1, 256]]), psf(ps, 0, [[1, 256]]), AF.Exp)
            S.memset("dve", A.b(o_Bm, [[1, 256]], p0=64, npart=64), 0.0)
            S.memset("dve", A.b(o_xde, [[1, 1024]], p0=64, npart=64), 0.0)
            for b in range(NB_S):
                st = o_st[b % 2]
                S.dma("sp", A.f(st, [[128, 8], [1, 128]]), dram(sssm, (l * NB_S + b) * 16 * 64 * 128, [[128, 128], [16384, 8], [1, 128]]))
                psT = [S.ps(), S.ps()]
                for half in range(2):
                    S.tr([(psf(psT[half], j * 128, [[1, 128]]), A.f(st + 4 * (half * 4 + j) * 128, [[1, 128]])) for j in range(4)], ident_f)
                Hb = o_H0b[b % 2]
                for half in range(2):
                    S.cp("act", A.b(Hb + 2 * half * 512, [[1, 512]]), psf(psT[half], 0, [[1, 512]]))
                S.tt("dve", A.b(o_CTm, [[64, 2], [1, 64]]), A.b(o_xc + 2 * 10 * Tn, [[Tn, 2], [1, 64]]),
                     A.b(o_colmask + 2 * b * 64, [[0, 2], [1, 64]]), ALU.mult)
                for g in range(2):
                    S.mm(psf(psyi[g], 0, [[1, 512]], npart=64), [(A.b(o_CTm + 2 * g * 64, [[1, 64]]), A.b(Hb + 2 * g * 512, [[1, 512]]))],
                         start=(b == 0), stop=(b == NB_S - 1))
                if cfg.get("st_lvl", 9) <= 1:
                    continue
                S.tt("dve", A.b(o_Bm, [[1, 256]], npart=64), A.b(o_btk, [[1, 256]], npart=64),
                     A.f(o_ssf + 4 * (128 + b), [[0, 256]], npart=64), ALU.mult)
                psS = [S.ps(), S.ps()]
                for g in range(2):
                    S.mm(psf(psS[g], 0, [[1, 512]]), [(A.b(o_Bm + 2 * g * 128, [[1, 128]]), A.b(o_xde + 2 * g * 512, [[1, 512]]))])
                for half in range(2):
                    if cfg.get("dbgA"):
                        continue
                    S.cp("act", A.f(o_hn + 4 * half * 512, [[1, 512]]), psf(psT[half], 0, [[1, 512]]))
                    S.tt("dve", A.f(o_hn + 4 * half * 512, [[64, 8], [1, 64]]), A.f(o_hn + 4 * half * 512, [[64, 8], [1, 64]]),
                         A.f(o_cdall + 4 * (b * 16 + half * 8), [[1, 8], [0, 64]]), ALU.mult)
                    hv = A.f(o_hn + 4 * half * 512, [[1, 512]])
                    if cfg.get("dbgB"):
                        continue
                    S.tt("dve", hv, hv, psf(psS[half], 0, [[1, 512]]), ALU.add)
                if cfg.get("st_lvl", 9) <= 2:
                    continue
                psB = [S.ps(), S.ps()]
                for half in range(2):
                    S.tr([(psf(psB[half], j * 128, [[1, 128]]), A.f(o_hn + 4 * (half * 4 + j) * 128, [[1, 128]])) for j in range(4)], ident_f)
                    S.cp("act", A.f(st + 4 * half * 512, [[1, 512]]), psf(psB[half], 0, [[1, 512]]))
                S.dma("pool", dram(ssm_s, (l * NB_S + b) * 1024 * 128, [[128, 128], [16384, 8], [1, 128]]), A.f(st, [[128, 8], [1, 128]]))
            for t_ in psyi:
                S.held.discard(int(t_.name[2:]))

        ssd_chunk(Tn, 0, 64, btri, bslow, o_ssf, hstate={"yinter": ssd_states, "update": lambda *a_: None})

        if sstop <= 3:
            return
        qk_slabs = [(0, [(False, 0), (False, 1), (False, 2), (False, 3)]),
                    (512, [(False, 4), (False, 5), (True, 0), (True, 1)]),
                    (1024, [(True, 2), (True, 3), (True, 4), (True, 5)])]
        for col0, items in qk_slabs:
            sl = slab("in", l, wb_in, 0, 8, IN_W, col0, 512)
            for j, (is_k, cc) in enumerate(items):
                qk_chunk(sl, j, is_k, cc, 0, Tn, False)

        def out_kv(tb):
            for g in range(3):
                for t in range(4):
                    S.dma("pool", dram(kvs[g], l * TS * 512 + t * 512, [[4 * 512, 16], [256, 2], [1, 256]]),
                          A.f(o_kvst + 4 * g * 256, [[768, 2], [1, 256]], p0=16 * t, npart=16))
            S.cp("act", A.b(o_Vn, [[1, 768]], npart=64), A.f(o_kvst + 4 * 768, [[1, 768]], npart=64))
        kv_natural(l, Tn, 1, 64, out_kv)
        psN = [S.ps(), S.ps()]
        mm_multi([(psf(psN[h % 2], (h // 2) * 64, [[1, 64]], npart=64),
                   A.b(o_kTn + 2 * (h // 2) * Tn, [[1, 64]], p0=64 * (h % 2), npart=64),
                   A.b(o_q + 2 * (h // 2) * Tn, [[1, 64]], p0=64 * (h % 2), npart=64)) for h in range(0, 12, 2)])
        mm_multi([(psf(psN[h % 2], (h // 2) * 64, [[1, 64]], npart=64),
                   A.b(o_kTn + 2 * (h // 2) * Tn, [[1, 64]], p0=64 * (h % 2), npart=64),
                   A.b(o_q + 2 * (h // 2) * Tn, [[1, 64]], p0=64 * (h % 2), npart=64)) for h in range(1, 12, 2)])
        for par_ in range(2):
            S.act(A.b(o_E[0], [[1, 384]], npart=64), psf(psN[par_], 0, [[1, 384]], npart=64), AF.Exp, scale=0.125)
            S.tt("dve", A.b(o_PnT + 2 * par_ * 64, [[128, 6], [1, 64]], npart=64), A.b(o_E[0], [[64, 6], [1, 64]], npart=64),
                 A.b(o_mnew + 2 * par_ * 64, [[128, 6], [1, 64]], npart=64), ALU.mult)
        if sstop <= 4:
            return
        for b in range(NB_S):
            S.dma("sp", A.f(o_Kc, [[1, 512]]), dram(ckv[0], (l * NB_S + b) * 128 * 512, [[512, 128], [1, 512]]))
            S.dma("sp", A.f(o_Kc + 4 * 512, [[512, 4], [1, 512]]), dram(ckv[1], (l * NB_S + b) * 512 * 512, [[4 * 512, 128], [512, 4], [1, 512]]))
            S.dma("sp", A.f(o_Kc + 4 * 5 * 512, [[512, 4], [1, 512]]), dram(ckv[2], (l * NB_S + b) * 2048 * 512, [[16 * 512, 128], [512, 4], [1, 512]]))
            S.cp("pool", A.b(o_Vcb, [[256, 9], [1, 256]]), A.f(o_Kc + 4 * 256, [[512, 9], [1, 256]]))
            allk = [(ch, hp) for ch in range(9) for hp in range(2)]
            for grp in range(5):
                items = allk[grp * 4:(grp + 1) * 4]
                ps = S.ps()
                S.tr([(psf(ps, n * 128, [[1, 128]]), A.f(o_Kc + 4 * (ch * 512 + hp * 128), [[1, 128]])) for n, (ch, hp) in enumerate(items)], ident_f)
                S.cp("act" if grp % 2 else "dve", A.b(o_KT + 2 * grp * 4 * 128, [[1, 128 * len(items)]]), psf(ps, 0, [[1, 128 * len(items)]]))
            psS2 = [S.ps(), S.ps()]
            for p_ in range(2):
                its = []
                for ch in range(9):
                    g = 0 if ch == 0 else (1 if ch < 5 else 2)
                    for hp in range(2):
                        its.append((psf(psS2[p_], (ch * 2 + hp) * 4, [[1, 4]]),
                                    A.b(o_KT + 2 * (ch * 2 + hp) * 128, [[1, 128]], p0=64 * p_, npart=64),
                                    A.b(o_q + 2 * ((2 * g + hp) * Tn + b), [[16, 4]], p0=64 * p_, npart=64)))
                mm_multi(its)
            for p_ in range(2):
                S.act(A.b(o_Es + 2 * p_ * 4, [[16, 9], [8, 2], [1, 4]]), psf(psS2[p_], 0, [[8, 9], [4, 2], [1, 4]]), AF.Exp, scale=0.125)
            S.tt("dve", A.b(o_Ps, [[1, 144]]), A.b(o_Es, [[1, 144]]), A.b(o_maskS, [[1, 144]]), ALU.mult)
            psUD = [S.ps(), S.ps()]
            for h in range(12):
                g, a = h // 4, h % 4
                chs = [0] if g == 0 else (list(range(1, 5)) if g == 1 else list(range(5, 9)))
                rhs_c = [A.b(o_Ps + 2 * (ch * 16 + a * 4), [[1, 4]]) for ch in chs]
                rhs_n = A.b(o_PnT + 2 * (h * 64 + b), [[16, 4]], npart=64)
                S.mm(psf(psUD[0], h * 4, [[1, 4]], npart=64),
                     [(A.b(o_Vcb + 2 * (ch * 256 + a * 64), [[1, 64]]), r_) for ch, r_ in zip(chs, rhs_c)]
                     + [(A.b(o_Vn + 2 * h * 64, [[1, 64]], npart=64), rhs_n)])
                S.mm(psf(psUD[1], h * 4, [[1, 4]], npart=64),
                     [(A.b(o_vones, [[1, 64]]), r_) for r_ in rhs_c] + [(A.b(o_vones, [[1, 64]], npart=64), rhs_n)])
            S.cp("act", A.f(o_ud, [[1, 48]], npart=64), psf(psUD[0], 0, [[1, 48]], npart=64))
            S.cp("dve", A.f(o_ud + 4 * 48, [[1, 48]], npart=64), psf(psUD[1], 0, [[1, 48]], npart=64))
            ds = A.f(o_ud + 4 * 96, [[1, 16]], npart=64)
            S.tt("dve", ds, A.f(o_ud + 4 * 48, [[1, 16]], npart=64), A.f(o_ud + 4 * 64, [[1, 16]], npart=64), ALU.add)
            S.tt("dve", ds, ds, A.f(o_ud + 4 * 80, [[1, 16]], npart=64), ALU.add)
            S.recip(ds, ds)
            S.tt("dve", A.f(o_ud, [[16, 3], [1, 16]], npart=64), A.f(o_ud, [[16, 3], [1, 16]], npart=64),
                 A.f(o_ud + 4 * 96, [[0, 3], [1, 16]], npart=64), ALU.mult)
            for par_ in range(2):
                S.cp("act" if par_ else "dve", A.b(o_mix + 2 * b, [[Tn, 6], [16, 4]], p0=64 * par_, npart=64),
                     A.f(o_ud + 4 * par_ * 4, [[8, 6], [1, 4]], npart=64))

        if sstop <= 5:
            return
        def fconv_sample(ch, ps, r):
            raw = o_uraw[r]
            S.cp("act", A.f(raw + 4 * 32, [[1, 64]]), psf(ps, 0, [[1, 64]]))
            S.cp("act", A.f(raw, [[1, 32]]), A.f(o_fhalos + 4 * ch * 32, [[1, 32]]))
            acc = A.f(o_facc[r], [[1, 64]])
            S.ts("dve", acc, A.f(raw + 4 * 32, [[1, 64]]), par(P_FCW + 3 * ch + 2, [[1, 1]]), par(P_FCB + ch, [[1, 1]]), ALU.mult, ALU.add)
            S.stt("dve", acc, A.f(raw + 4 * 16, [[1, 64]]), par(P_FCW + 3 * ch + 1, [[1, 1]]), acc, ALU.mult, ALU.add)
            S.stt("dve", acc, A.f(raw, [[1, 64]]), par(P_FCW + 3 * ch, [[1, 1]]), acc, ALU.mult, ALU.add)
            return acc

        def up_tok(js, slot):
            ps = S.ps()
            S.mm(psf(ps, 0, [[1, 512]], npart=64), [(A.b(o_hT + 2 * k * Tn, [[1, 64]]), wv(slot, k, 0, 512, 512)) for k in range(8)])
            sg_ = o_ytok[js % 2]
            S.cp("act", A.f(sg_, [[1, 512]], npart=64), psf(ps, 0, [[1, 512]], npart=64))
            for r in range(2):
                for part in range(2):
                    S.dma("pool", dram(ffn_s, (l * NB_S * 2 + r) * 2 * DFF + part * DFF + 256 * js, [[2 * 2 * DFF, 16], [1, 256]]),
                          A.f(sg_ + 4 * part * 256, [[1, 256]], p0=16 * (r + 2), npart=16))
        p2(l, Tn, o_xTs, fconv_sample, up_tok)
        if l == nlayers - 1:
            yt = o_ytok[0]
            for half in range(2):
                ps = S.ps()
                S.tr([(psf(ps, j * 128, [[1, 128]], npart=64), A.f(o_xTs + 4 * (half * 4 + j) * Tn, [[1, Tn]])) for j in range(4)], ident_f)
                S.cp("act", A.f(yt + 4 * half * 512, [[1, 512]], npart=64), psf(ps, 0, [[1, 512]], npart=64))
            for t in range(4):
                S.dma("pool", dram(y_s, t * D, [[4 * D, 16], [1, D]]), A.f(yt, [[1, 1024]], p0=16 * t, npart=16))

    stop = cfg.get("stop", 99)
    for l in range(nlayers):
        load_params(l)
        if stop <= 1:
            break
        S.memset("dve", A.f(o_halo, [[1, 36]]), 0.0)
        S.memset("dve", A.f(o_fhalo, [[1, 88]]), 0.0)
        S.memset("dve", A.f(o_H, [[1, 1024]]), 0.0)
        S.memset("dve", A.b(o_Hb, [[1, 1024]]), 0.0)
        for i in range(ntiles):
            load_x_prompt(l, i)
            rmsnorm_fm(o_xT, T, P_GMIX)
            if stop <= 2:
                continue
            p1a_proj(l, T, 4, 128, conv_prompt)
            if stop <= 3:
                continue
            for tb in range(4):
                ssd_chunk(T, tb, 128, tri_f, slow_f, o_cf + 512)
            if stop <= 4:
                continue
            p1b_prompt(l, i)
            if stop <= 5:
                continue
            p2(l, T, o_xT, fconv_prompt)
            store_prompt(l, i)
        layer_end_prompt(l)
        if do_sample:
            sample_tile(l)
    S.finish()
    S.emit()
    return nc


_CACHE = {}


def run(inputs, cfg):
    key = tuple(sorted(cfg.items()))
    if key not in _CACHE:
        _CACHE[key] = build(cfg)
    nc = _CACHE[key]
    hc = host_consts()
    f = lambda a: np.ascontiguousarray(a, dtype=np.float32)
    in_maps = []
    ncores = cfg.get("ncores", NCORES)
    for c in range(ncores):
        b0 = NB_S * c
        m = {"x_p": f(inputs["x_prompt"][c // 2]),
             "x_s": f(inputs["x_sample"][b0:b0 + NB_S].transpose(1, 0, 2).reshape(TS, D)),
             "ckv0": f(inputs["cache_kv0"][:, b0:b0 + NB_S].reshape(2, NB_S, 128, 512)),
             "ckv1": f(inputs["cache_kv1"][:, b0:b0 + NB_S].reshape(2, NB_S, 512, 512)),
             "ckv2": f(inputs["cache_kv2"][:, b0:b0 + NB_S].reshape(2, NB_S, 2048, 512)),
             "sssm": f(inputs["state_ssm"][:, b0:b0 + NB_S]),
             "sconv": f(inputs["state_conv"][:, b0:b0 + NB_S]),
             "sffn": f(inputs["state_ffn_conv"][:, b0:b0 + NB_S])}
        for n in WNAMES:
            m[n] = f(inputs[n])
        m.update(hc)
        in_maps.append(m)
    res = run_bass_kernel_spmd(nc, in_maps, core_ids=list(range(ncores))).results
    res = list(res) + [res[0]] * (NCORES - ncores)
    ev = [res[2 * b] for b in range(4)]
    y_prompt = np.stack([r["y_p"] for r in ev])
    y_sample = np.concatenate([r["y_s"].reshape(NB_S, 4, D) for r in res], axis=0)
    kvp = [np.stack([r[f"kv{g}_p"].reshape(2, Wg, 2, 4, 64) for r in ev], axis=1) for g, Wg in enumerate((128, 512, 2048))]
    ssm_p = np.stack([r["ssm_p"].reshape(2, 16, 64, 128) for r in ev], axis=1)
    conv_p = np.stack([r["conv_p"] for r in ev], axis=1)
    ffn_p = np.stack([r["ffn_p"] for r in ev], axis=1)
    kvs = [np.concatenate([r[f"kv{g}_s"].reshape(2, NB_S, 4, 2, 4, 64) for r in res], axis=1) for g in range(3)]
    ssm_s = np.concatenate([r["ssm_s"].reshape(2, NB_S, 16, 64, 128) for r in res], axis=1)
    conv_s = np.concatenate([r["conv_s"] for r in res], axis=1)
    ffn_s = np.concatenate([r["ffn_s"] for r in res], axis=1)
    return (y_prompt, y_sample, kvp[0], kvp[1], kvp[2], ssm_p, conv_p, ffn_p,
            kvs[0], kvs[1], kvs[2], ssm_s, conv_s, ffn_s)


def kernel(**inputs):
    return run(inputs, {})
```
